# Optimizing a Trainium2 kernel written in Bass

```python
import jax, jax.numpy as jnp
from jax import lax
import numpy as np

D_MODEL = 2048
BATCH = 8
SEQ = 4096
DEPTH = 2
DEC_BATCH = 8
DEC_SEQ = 16
PAST_LEN = 1024

CHUNK = 64
N_MIXERS = 2
H_A = 16
DH_A = D_MODEL // H_A
N_PREV_A = 8
REL_CLIP = 128
DH_B = 64
H_B = D_MODEL // DH_B
KV_B = H_B // 4
G_B = H_B // KV_B
WINDOW_B = 128
N_PREV_B = WINDOW_B // CHUNK
ROT_DIM = DH_B // 4
ROPE_THETA = 500000.0
D_FF = 4 * D_MODEL
ALPHA = (2 * DEPTH) ** 0.25
BETA = (8 * DEPTH) ** -0.25
LN_EPS = 1e-5
NEG_INF = -1e30

kernel_name = 'hybrid_streaming_encoder_step'


def layer_norm(x, g, b):
    xf = x.astype(jnp.float32)
    mu = xf.mean(-1, keepdims=True)
    var = jnp.square(xf - mu).mean(-1, keepdims=True)
    return ((xf - mu) * lax.rsqrt(var + LN_EPS)).astype(x.dtype) * g + b


def ada_modulation(c, w, b):
    mod = jax.nn.silu(c) @ w + b
    shift, scale, gate = jnp.split(mod, 3, axis=-1)
    return shift[:, None], scale[:, None], gate[:, None]


def pad_to_chunks(x):
    s = x.shape[1]
    s_pad = -(-s // CHUNK) * CHUNK
    return jnp.pad(x, ((0, 0), (0, s_pad - s)) + ((0, 0),) * (x.ndim - 2))


def extend_with_history(new, hist, n_prev):
    s = new.shape[1]
    s_pad = -(-s // CHUNK) * CHUNK
    hist_len = 0 if hist is None else hist.shape[1]
    body = new if hist is None else jnp.concatenate([hist, new], axis=1)
    front = n_prev * CHUNK - hist_len
    ext = jnp.pad(body, ((0, 0), (front, s_pad - s), (0, 0), (0, 0)))
    r = jnp.arange(ext.shape[1])
    valid = (r >= front) & (r < front + hist_len + s)
    return ext, valid


def rel_position_bias(table, n_prev):
    i = jnp.arange(CHUNK)[:, None]
    j = jnp.arange((n_prev + 1) * CHUNK)[None, :]
    idx = jnp.clip(n_prev * CHUNK + i - j, -REL_CLIP, REL_CLIP) + REL_CLIP
    return table[:, idx]


def band_attention(q, k_ext, v_ext, valid, n_prev, bias=None, sink=None):
    b, s_pad, hkv, g, dh = q.shape
    nc = s_pad // CHUNK
    band = (n_prev + 1) * CHUNK
    scale = dh ** -0.5
    qc = q.reshape(b, nc, CHUNK, hkv, g, dh).transpose(1, 0, 2, 3, 4, 5)

    def one_chunk(args):
        c, qb = args
        start = c * CHUNK
        kb = lax.dynamic_slice_in_dim(k_ext, start, band, axis=1)
        vb = lax.dynamic_slice_in_dim(v_ext, start, band, axis=1)
        ok = lax.dynamic_slice_in_dim(valid, start, band, axis=0)
        s = jnp.einsum('bqkgd,bjkd->bkgqj', qb, kb).astype(jnp.float32) * scale
        if bias is not None:
            s = s + bias.astype(jnp.float32)
        s = jnp.where(ok[None, None, None, None, :], s, NEG_INF)
        if sink is None:
            p = jax.nn.softmax(s, axis=-1)
        else:
            sk = sink.astype(jnp.float32)[None, :, :, None, None]
            m = jnp.maximum(s.max(-1, keepdims=True), sk)
            e = jnp.exp(s - m)
            p = e / (e.sum(-1, keepdims=True) + jnp.exp(sk - m))
        return jnp.einsum('bkgqj,bjkd->bqkgd', p.astype(vb.dtype), vb)

    out = lax.map(one_chunk, (jnp.arange(nc), qc))
    return out.transpose(1, 0, 2, 3, 4, 5).reshape(b, s_pad, hkv * g * dh)


def partial_rope(x, pos):
    half = ROT_DIM // 2
    freqs = ROPE_THETA ** (-jnp.arange(0, ROT_DIM, 2, dtype=jnp.float32) / ROT_DIM)
    ang = pos.astype(jnp.float32)[:, None] * freqs[None, :]
    cos = jnp.cos(ang)[None, :, None, :]
    sin = jnp.sin(ang)[None, :, None, :]
    xr = x[..., :ROT_DIM].astype(jnp.float32)
    x1, x2 = xr[..., :half], xr[..., half:]
    rot = jnp.concatenate([x1 * cos - x2 * sin, x2 * cos + x1 * sin], axis=-1).astype(x.dtype)
    return jnp.concatenate([rot, x[..., ROT_DIM:]], axis=-1)


def mixer_a(h, hist_k, hist_v, w_qkv, w_o, rel_table):
    b, s, _ = h.shape
    q, k, v = jnp.split(h @ w_qkv, 3, axis=-1)
    q = q.reshape(b, s, H_A, DH_A)
    k = k.reshape(b, s, H_A, DH_A)
    v = v.reshape(b, s, H_A, DH_A)
    k_ext, valid = extend_with_history(k, hist_k, N_PREV_A)
    v_ext, _ = extend_with_history(v, hist_v, N_PREV_A)
    bias = rel_position_bias(rel_table, N_PREV_A)[:, None]
    o = band_attention(pad_to_chunks(q)[:, :, :, None, :], k_ext, v_ext, valid, N_PREV_A, bias=bias)
    return o[:, :s] @ w_o, k, v


def mixer_b(h, pos0, hist_k, hist_v, w_qkv, w_o, sink):
    b, s, _ = h.shape
    q, k, v = jnp.split(h @ w_qkv, [H_B * DH_B, H_B * DH_B + KV_B * DH_B], axis=-1)
    pos = pos0 + jnp.arange(s)
    q = partial_rope(q.reshape(b, s, H_B, DH_B), pos)
    k = partial_rope(k.reshape(b, s, KV_B, DH_B), pos)
    v = v.reshape(b, s, KV_B, DH_B)
    k_ext, valid = extend_with_history(k, hist_k, N_PREV_B)
    v_ext, _ = extend_with_history(v, hist_v, N_PREV_B)
    qg = pad_to_chunks(q).reshape(b, -1, KV_B, G_B, DH_B)
    o = band_attention(qg, k_ext, v_ext, valid, N_PREV_B, sink=sink.reshape(KV_B, G_B))
    return o[:, :s] @ w_o, k, v


def trunk(x, c, pos0, hist_a_k, hist_a_v, hist_b_k, hist_b_v,
          w_ada, b_ada, ln_g, ln_b, w_qkv_a, w_o_a, rel_bias_a,
          w_qkv_b, w_o_b, sink_b, w_up, w_down):
    new_a_k, new_a_v, new_b_k, new_b_v = [], [], [], []
    for i in range(DEPTH):
        l = i // N_MIXERS
        shift, scale, gate = ada_modulation(c, w_ada[i, 0], b_ada[i, 0])
        h = x * (1 + scale) + shift
        if i % N_MIXERS == 0:
            hk = None if hist_a_k is None else hist_a_k[l]
            hv = None if hist_a_v is None else hist_a_v[l]
            y, k, v = mixer_a(h, hk, hv, w_qkv_a[l], w_o_a[l], rel_bias_a[l])
            new_a_k.append(k[:, -N_PREV_A * CHUNK:])
            new_a_v.append(v[:, -N_PREV_A * CHUNK:])
        else:
            hk = None if hist_b_k is None else hist_b_k[l]
            hv = None if hist_b_v is None else hist_b_v[l]
            y, k, v = mixer_b(h, pos0, hk, hv, w_qkv_b[l], w_o_b[l], sink_b[l])
            new_b_k.append(k[:, -N_PREV_B * CHUNK:])
            new_b_v.append(v[:, -N_PREV_B * CHUNK:])
        x = layer_norm(ALPHA * x + gate * y, ln_g[i, 0], ln_b[i, 0])
        shift, scale, gate = ada_modulation(c, w_ada[i, 1], b_ada[i, 1])
        h = x * (1 + scale) + shift
        f = jnp.square(jax.nn.relu(h @ w_up[i])) @ w_down[i]
        x = layer_norm(ALPHA * x + gate * f, ln_g[i, 1], ln_b[i, 1])
    return x, jnp.stack(new_a_k), jnp.stack(new_a_v), jnp.stack(new_b_k), jnp.stack(new_b_v)


def setup_inputs(seed: int = 0) -> dict:
    key = jax.random.key(seed)
    ks = jax.random.split(key, 20)
    n_a = (DEPTH + 1) // 2
    n_b = DEPTH // 2
    la = min(N_PREV_A * CHUNK, PAST_LEN)
    lb = min(N_PREV_B * CHUNK, PAST_LEN)
    d = D_MODEL

    def nrm(k, shape, s):
        return jax.random.normal(k, shape, jnp.float32) * s

    return {
        'x_prompt': nrm(ks[0], (BATCH, SEQ, d), 1.0),
        'x_sample': nrm(ks[1], (DEC_BATCH, DEC_SEQ, d), 1.0),
        'cache_a_k': nrm(ks[2], (n_a, DEC_BATCH, la, H_A, DH_A), 1.0),
        'cache_a_v': nrm(ks[3], (n_a, DEC_BATCH, la, H_A, DH_A), 1.0),
        'cache_b_k': nrm(ks[4], (n_b, DEC_BATCH, lb, KV_B, DH_B), 1.0),
        'cache_b_v': nrm(ks[5], (n_b, DEC_BATCH, lb, KV_B, DH_B), 1.0),
        'c_prompt': nrm(ks[6], (BATCH, d), 1.0),
        'c_sample': nrm(ks[7], (DEC_BATCH, d), 1.0),
        'w_ada': nrm(ks[8], (DEPTH, 2, d, 3 * d), 0.5 * d ** -0.5),
        'b_ada': nrm(ks[9], (DEPTH, 2, 3 * d), 0.02),
        'ln_g': 1.0 + nrm(ks[10], (DEPTH, 2, d), 0.05),
        'ln_b': nrm(ks[11], (DEPTH, 2, d), 0.02),
        'w_qkv_a': nrm(ks[12], (n_a, d, 3 * H_A * DH_A), d ** -0.5),
        'w_o_a': nrm(ks[13], (n_a, H_A * DH_A, d), BETA * (H_A * DH_A) ** -0.5),
        'rel_bias_a': nrm(ks[14], (n_a, H_A, 2 * REL_CLIP + 1), 0.5),
        'w_qkv_b': nrm(ks[15], (n_b, d, H_B * DH_B + 2 * KV_B * DH_B), d ** -0.5),
        'w_o_b': nrm(ks[16], (n_b, H_B * DH_B, d), BETA * (H_B * DH_B) ** -0.5),
        'sink_b': nrm(ks[17], (n_b, H_B), 0.5),
        'w_up': nrm(ks[18], (DEPTH, d, D_FF), d ** -0.5),
        'w_down': nrm(ks[19], (DEPTH, D_FF, d), BETA * D_FF ** -0.5),
    }


def reference(x_prompt, x_sample, cache_a_k, cache_a_v, cache_b_k, cache_b_v, c_prompt, c_sample,
              w_ada, b_ada, ln_g, ln_b, w_qkv_a, w_o_a, rel_bias_a, w_qkv_b, w_o_b, sink_b, w_up, w_down):
    y_prompt, ak_p, av_p, bk_p, bv_p = trunk(
        x_prompt, c_prompt, 0, None, None, None, None,
        w_ada, b_ada, ln_g, ln_b, w_qkv_a, w_o_a, rel_bias_a, w_qkv_b, w_o_b, sink_b, w_up, w_down)
    y_sample, ak_s, av_s, bk_s, bv_s = trunk(
        x_sample, c_sample, PAST_LEN, cache_a_k, cache_a_v, cache_b_k, cache_b_v,
        w_ada, b_ada, ln_g, ln_b, w_qkv_a, w_o_a, rel_bias_a, w_qkv_b, w_o_b, sink_b, w_up, w_down)
    return (y_prompt, y_sample, ak_p, av_p, bk_p, bv_p, ak_s, av_s, bk_s, bv_s)
```

```python
import numpy as np
import concourse.bass as bass
import concourse.mybir as mybir
from concourse.bass_utils import run_bass_kernel_spmd

F32 = mybir.dt.float32
BF16 = mybir.dt.bfloat16
AF = mybir.ActivationFunctionType
ALU = mybir.AluOpType

D = 2048
KC = 16
TT = 512
ALPHA = float(4.0 ** 0.25)
EPS = 1e-5
NBUF = 3
NADA = 96
NMAIN = 182
NGRP = NADA + NMAIN
SG = 14
ROT = 16
THETA = 500000.0


class Eng:
    def __init__(self, e, name, sem, is_pe=False):
        self.e = e
        self.name = name
        self.sem = sem
        self.cnt = 0
        self.seen = {}
        self.is_pe = is_pe


class Chan:
    def __init__(self, sem, name):
        self.sem = sem
        self.cnt = 0
        self.name = name


PASS_MARK = []


class StopBuild(Exception):
    pass


import os as _os
_STOP = int(_os.environ.get("K_STOP", "99"))
_SUB = int(_os.environ.get("K_SUB", "99"))


def ckpt(n):
    if _STOP <= n:
        raise StopBuild()


class Ctx:
    def __init__(self, nc):
        self.nc = nc
        self.st = {}
        self.sems = []

    def sem(self, name):
        s = self.nc.alloc_semaphore(name)
        self.sems.append(s)
        return s

    def _wait(self, E, reads, writes):
        deps = {}
        for k in reads:
            st = self.st.get(k)
            if st and st[0] is not None:
                s, v = st[0]
                if deps.get(s, 0) < v:
                    deps[s] = v
        for k in writes:
            st = self.st.get(k)
            if st:
                if st[0] is not None:
                    s, v = st[0]
                    if deps.get(s, 0) < v:
                        deps[s] = v
                for (s, v) in st[1]:
                    if deps.get(s, 0) < v:
                        deps[s] = v
        for s, v in deps.items():
            if s is E and E.is_pe:
                continue
            if E.seen.get(s, 0) < v:
                E.e.wait_ge(s.sem, v)
                E.seen[s] = v

    def _upd(self, reads, writes, stamp):
        for k in reads:
            st = self.st.get(k)
            if st is None:
                self.st[k] = [None, [stamp]]
            else:
                st[1] = [x for x in st[1] if x[0] is not stamp[0]] + [stamp]
        for k in writes:
            self.st[k] = [stamp, []]

    def op(self, E, fn, reads=(), writes=(), signal=True):
        self._wait(E, reads, writes)
        ins = fn()
        E.nops = getattr(E, 'nops', 0) + 1
        if signal:
            ins.then_inc(E.sem, 1)
            E.cnt += 1
            stamp = (E, E.cnt)
        else:
            stamp = (E, E.cnt + 1)
        self._upd(reads, writes, stamp)
        return ins

    def dma(self, Q, chan, out, in_, reads=(), writes=(), **kw):
        self._wait(Q, reads, writes)
        ins = Q.e.dma_start(out=out, in_=in_, **kw)
        ins.then_inc(chan.sem, 16)
        chan.cnt += 16
        self._upd(reads, writes, (chan, chan.cnt))
        return ins


def stat_groups(W):
    K, N = W.shape
    kc = K // 128
    assert kc == 16 and N % 256 == 0
    G = N // 256
    a = W.reshape(kc, 128, G, 2, 128)
    a = a.transpose(2, 1, 3, 0, 4)
    return np.ascontiguousarray(a).reshape(G, 128, 4096)


def mov_groups(W):
    K, N = W.shape
    kc = K // 128
    G = N // 256
    a = W.reshape(kc, 128, G, 256)
    a = a.transpose(2, 1, 0, 3)
    return np.ascontiguousarray(a).reshape(G, 128, 4096)


def down_groups(W, half):
    Wh = W[half * 4096:(half + 1) * 4096]
    a = Wh.reshape(32, 128, 16, 128)
    a = a.transpose(2, 1, 0, 3)
    return np.ascontiguousarray(a).reshape(16, 128, 4096)


def build_weights(w_ada, w_qkv_a, w_o_a, w_qkv_b, w_o_b, w_up, w_down):
    gs = []
    for u in range(4):
        gs.append(stat_groups(w_ada[u // 2, u % 2]))
    qa = w_qkv_a[0]
    gs.append(stat_groups(qa[:, 0:2048]))
    gs.append(stat_groups(qa[:, 2048:4096]))
    gs.append(mov_groups(qa[:, 4096:6144]))
    gs.append(stat_groups(w_o_a[0]))
    for h in range(2):
        gs.append(stat_groups(w_up[0][:, h * 4096:(h + 1) * 4096]))
        gs.append(down_groups(w_down[0], h))
    qb = w_qkv_b[0]
    gs.append(stat_groups(qb[:, 0:2048]))
    kb = qb[:, 2048:2560].reshape(2048, 8, 1, 64)
    kbd = np.broadcast_to(kb, (2048, 8, 2, 64)).reshape(2048, 1024)
    gs.append(stat_groups(kbd))
    gs.append(mov_groups(qb[:, 2560:3072]))
    gs.append(stat_groups(w_o_b[0]))
    for h in range(2):
        gs.append(stat_groups(w_up[1][:, h * 4096:(h + 1) * 4096]))
        gs.append(down_groups(w_down[1], h))
    wall = np.concatenate(gs, axis=0)
    assert wall.shape == (NGRP, 128, 4096), wall.shape
    return wall


def rope_tables():
    half = ROT // 2
    freqs = (np.float32(THETA) ** (-np.arange(0, ROT, 2, dtype=np.float32) / np.float32(ROT))).astype(np.float32)
    C = np.ones((9, 128, TT), np.float32)
    S = np.zeros((9, 128, TT), np.float32)
    for t in range(9):
        pos0 = t * TT if t < 8 else 1024
        pos = (pos0 + np.arange(TT)).astype(np.float32)
        ang = (pos[:, None] * freqs[None, :]).astype(np.float32)
        cs = np.cos(ang).astype(np.float32).T
        sn = np.sin(ang).astype(np.float32).T
        for hb in (0, 64):
            C[t, hb:hb + 8] = cs
            C[t, hb + 8:hb + 16] = cs
            S[t, hb:hb + 8] = -sn
            S[t, hb + 8:hb + 16] = sn
    permT = np.zeros((128, 128), np.float32)
    for m in range(128):
        d = m % 64
        if d < 8:
            permT[m + 8, m] = 1.0
        elif d < 16:
            permT[m - 8, m] = 1.0
    return C, S, permT


def build(NT=8, DO_SAMPLE=True):
    nc = bass.Bass("TRN2", target_bir_lowering=False)
    cx = Ctx(nc)

    def din(name, shape, dt=F32):
        return nc.dram_tensor(name, list(shape), dt, kind="ExternalInput")

    def dout(name, shape):
        return nc.dram_tensor(name, list(shape), F32, kind="ExternalOutput")

    xp = din("xp", [4096, 2048])
    xs = din("xs", [16, 2048])
    cT_d = din("cT", [128, 32])
    cak = din("cak", [512, 2048])
    cav = din("cav", [512, 2048])
    cbk = din("cbk", [128, 512])
    cbv = din("cbv", [128, 512])
    wall = din("wall", [NGRP, 128, 4096])
    lnT_d = din("lnT", [128, 128])
    bada_d = din("badaT", [128, 192])
    G_d = din("G", [16, 128, 320])
    tabc_d = din("tabc", [128, 16])
    sink_d = din("sinkT", [128, 16])
    ident_d = din("ident", [128, 128])
    perm_d = din("permT", [128, 128])
    ropeC_d = din("ropeC", [9, 128, TT])
    ropeS_d = din("ropeS", [9, 128, TT])
    wscr_m = nc.dram_tensor("wscr_m", [NMAIN, 128, 4096], BF16, kind="Internal")
    Gscr = nc.dram_tensor("Gscr", [16, 128, 640], BF16, kind="Internal")

    def wscr_g(g):
        assert g >= NADA
        return wscr_m[g - NADA]

    y_p = dout("y_p", [4096, 2048])
    y_s = dout("y_s", [16, 2048])
    ak_p = dout("ak_p", [512, 2048])
    av_p = dout("av_p", [512, 2048])
    bk_p = dout("bk_p", [128, 512])
    bv_p = dout("bv_p", [128, 512])
    ak_s = dout("ak_s", [16, 2048])
    av_s = dout("av_s", [16, 2048])
    bk_s = dout("bk_s", [16, 512])
    bv_s = dout("bv_s", [16, 512])

    def sb(name, shape, dt):
        return nc.alloc_sbuf_tensor("sb_" + name, list(shape), dt)

    xa = sb("xa", [128, 8192], F32)
    R1 = sb("R1", [128, 8192], BF16)
    R2f = sb("R2f", [128, 4096], F32)
    R2 = R2f.bitcast(BF16)
    KA = sb("KA", [128, 16384], BF16)
    VA = sb("VA", [128, 16384], BF16)
    KB = sb("KB", [128, 8 * 640], BF16)
    VB = sb("VB", [128, 5 * 512], BF16)
    wbuf = [sb(f"wbuf{i}", [128, 4096], BF16) for i in range(NBUF)]
    ident = sb("ident", [128, 128], F32)
    permT = sb("permT", [128, 128], F32)
    onesD = sb("onesD", [128, 128], F32)
    onesb = sb("onesb", [128, 128], BF16)
    zb = sb("zb", [128, 512], BF16)
    ropeC = sb("ropeC", [128, TT], F32)
    ropeS = sb("ropeS", [128, TT], F32)
    s1 = sb("s1", [128, TT], F32)
    s2 = sb("s2", [128, TT], F32)
    msb = sb("msb", [128, TT], F32)
    rstd = sb("rstd", [128, TT], F32)
    nmr = sb("nmr", [128, TT], F32)
    rden = nmr
    NWK = 4
    wk = [sb(f"wk{i}", [128, TT], F32) for i in range(NWK)]
    NPT = 6
    LOOK = 5
    PT = [sb(f"PT{i}", [128, TT], BF16) for i in range(NPT)]
    Gf = sb("Gf", [128, 320], F32)
    Ghl = [sb(f"Ghl{i}", [128, 640], BF16) for i in range(2)]
    identb = sb("identb", [128, 128], BF16)
    permTb = sb("permTb", [128, 128], BF16)
    tabc = sb("tabc", [128, 16], F32)
    sinkT = sb("sinkT", [128, 16], F32)
    esink = sb("esink", [128, 16], F32)
    lnT = sb("lnT", [128, 128], F32)
    badaT = sb("badaT", [128, 192], F32)
    cT = sb("cT", [128, 32], F32)
    scT = sb("scT", [128, 32], BF16)
    sig = sb("sig", [128, 32], F32)
    MOD = sb("MOD", [128, 4 * 2 * 48], F32)
    SC = sb("SC", [128, 32 * 16], F32)
    epsT = sb("epsT", [128, 1], F32)

    ps = [nc.alloc_psum_tensor(f"ps{i}", [128, 512], F32) for i in range(8)]

    PE = Eng(nc.tensor, "pe", cx.sem("s_pe"), is_pe=True)
    ACT = Eng(nc.scalar, "act", cx.sem("s_act"))
    DVE = Eng(nc.vector, "dve", cx.sem("s_dve"))
    POOL = Eng(nc.gpsimd, "pool", cx.sem("s_pool"))
    SP = Eng(nc.sync, "sp", None)

    def chan(name):
        return Chan(cx.sem(name), name)

    wchan = [chan(f"c_w{i}") for i in range(NBUF)]
    c_const = chan("c_const")
    c_scr = [chan(f"c_scr{i}") for i in range(NBUF)]
    c_wsw = [chan(f"c_wsw{i}") for i in range(NBUF)]
    c_stg = [chan("c_stg0"), chan("c_stg1")]
    c_G = [chan("c_G0"), chan("c_G1")]
    c_Gf = chan("c_Gf")
    c_Gs = [chan("c_Gs0"), chan("c_Gs1")]
    c_ropeC = chan("c_ropeC")
    c_ropeS = chan("c_ropeS")
    c_out = chan("c_out")
    c_wkout = [chan(f"c_wko{i}") for i in range(NWK)]
    c_cache = chan("c_cache")

    wk_i = [0]

    def next_wk():
        i = wk_i[0] % NWK
        wk_i[0] += 1
        return i

    pt_i = [0]

    def next_pt():
        i = pt_i[0] % NPT
        pt_i[0] += 1
        return i

    bank_lo = [0]

    def next_bank():
        b = bank_lo[0] % 4
        bank_lo[0] += 1
        return b

    def A(t, a, b):
        return t[:, a:b]

    def mm(out, lhsT, rhs, start, stop, reads, writes, signal):
        return cx.op(PE, lambda: nc.tensor.matmul(out, lhsT, rhs, start=start, stop=stop),
                     reads=reads, writes=writes, signal=signal)

    def act(out, in_, func, reads, writes, bias=None, scale=None):
        kw = {}
        if bias is not None:
            kw["bias"] = bias
        if scale is not None:
            kw["scale"] = scale
        return cx.op(ACT, lambda: nc.scalar.activation(out=out, in_=in_, func=func, **kw),
                     reads=reads, writes=writes)

    def tt(E, out, in0, in1, op, reads, writes):
        return cx.op(E, lambda: E.e.tensor_tensor(out=out, in0=in0, in1=in1, op=op),
                     reads=reads, writes=writes)

    def tcopy(E, out, in_, reads, writes):
        return cx.op(E, lambda: E.e.tensor_copy(out=out, in_=in_), reads=reads, writes=writes)

    def stt(out, in0, scalar, in1, op0, op1, reads, writes):
        return cx.op(DVE, lambda: nc.vector.scalar_tensor_tensor(out=out, in0=in0, scalar=scalar, in1=in1,
                                                                 op0=op0, op1=op1),
                     reads=reads, writes=writes)

    def ts(out, in0, s1_, op0, reads, writes, s2_=None, op1=None):
        if op1 is None:
            return cx.op(DVE, lambda: nc.vector.tensor_scalar(out=out, in0=in0, scalar1=s1_, scalar2=None, op0=op0),
                         reads=reads, writes=writes)
        return cx.op(DVE, lambda: nc.vector.tensor_scalar(out=out, in0=in0, scalar1=s1_, scalar2=s2_, op0=op0, op1=op1),
                     reads=reads, writes=writes)

    npass = NT + (1 if DO_SAMPLE else 0)
    main_g = list(range(NADA, NGRP))

    def ada_g(u):
        return list(range(u * 24, (u + 1) * 24))
    order = ada_g(0) + main_g[0:24] + ada_g(1) + main_g[24:32] + ada_g(2) + main_g[32:110] + ada_g(3) + main_g[110:182]
    for _ in range(npass - 1):
        order += main_g
    ws = {"load": 0, "use": 0}
    cast_seq = []
    _seen = set()
    for g in order:
        if g not in _seen:
            _seen.add(g)
            cast_seq.append(g)
    cast_pos = {g: i for i, g in enumerate(cast_seq)}

    def cast_key(g):
        return ("wscr", cast_pos[g] // SG)

    NFIRST = NADA + NMAIN

    def ws_load():
        i = ws["load"]
        if i >= len(order):
            return
        g = order[i]
        b = i % NBUF
        if i < NFIRST:
            cx.dma(POOL, c_wsw[b], out=wbuf[b][:, :].rearrange("p (a b) -> p a b", b=2048),
                   in_=wall[g].rearrange("p (a b) -> p a b", b=2048), writes=[("wbuf", b)])
            if g >= NADA and npass > 1:
                cx.dma(SP, c_scr[b], out=wscr_g(g), in_=wbuf[b][:, :], reads=[("wbuf", b)], writes=[("wscr", g)])
        else:
            cx.dma(SP, wchan[b], out=wbuf[b][:, :], in_=wscr_g(g), reads=[("wscr", g)], writes=[("wbuf", b)])
        ws["load"] += 1

    def ws_use():
        b = ws["use"] % NBUF
        ws["use"] += 1
        return b

    def cload(dst, src, key):
        cx.dma(SP, c_const, out=dst, in_=src, writes=[key])

    cload(ident[:, :], ident_d[:, :], "ident")
    cload(permT[:, :], perm_d[:, :], "permT")
    cload(lnT[:, :], lnT_d[:, :], "lnT")
    cload(badaT[:, :], bada_d[:, :], "badaT")
    cload(cT[:, :], cT_d[:, :], "cT")
    cload(tabc[:, :], tabc_d[:, :], "tabc")
    cload(sinkT[:, :], sink_d[:, :], "sinkT")
    for k_ in ["ident", "permT", "lnT", "badaT", "cT", "tabc", "sinkT"]:
        cx.st[k_] = [(c_const, c_const.cnt), []]

    STOPPED = [False]
    try:
        ckpt(0)
    except StopBuild:
        STOPPED[0] = True
    if not STOPPED[0]:
        for _ in range(NBUF):
            ws_load()

    cx.op(DVE, lambda: nc.vector.memset(onesD[:, :], 1.0 / D), writes=["onesD"])
    cx.op(DVE, lambda: nc.vector.memset(onesb[:, :], 1.0), writes=["onesb"])
    cx.op(DVE, lambda: nc.vector.memset(zb[:, :], 0.0), writes=["zb"])
    cx.op(DVE, lambda: nc.vector.memset(epsT[:, :], EPS), writes=["epsT"])

    act(sig[:, :], cT[:, :], AF.Sigmoid, reads=["cT"], writes=["sig"])
    tt(DVE, scT[:, :], sig[:, :], cT[:, :], ALU.mult, reads=["sig", "cT"], writes=["scT"])
    act(esink[:, :], sinkT[:, :], AF.Exp, reads=["sinkT"], writes=["esink"])

    def modv(u, j, part):
        o = (u * 2 + j) * 48 + part * 16
        return MOD[:, o:o + 16]

    slot_i = [0]

    def new_slot():
        i = slot_i[0]
        slot_i[0] += 1
        return SC[:, i * 16:(i + 1) * 16]

    def lnG(u):
        return lnT[:, u * 16:(u + 1) * 16]

    def lnB(u):
        return lnT[:, 64 + u * 16:64 + (u + 1) * 16]

    OPS = {}
    HA = {}
    HB = {}
    XA = {}
    XB = {}
    A0 = {}
    for u in range(4):
        for j in range(2):
            OPS[(u, j)] = new_slot()
            if u >= 1:
                HA[(u, j)] = new_slot()
                HB[(u, j)] = new_slot()
        XA[u] = new_slot()
        XB[u] = new_slot()
    for j in range(2):
        A0[j] = new_slot()
    for u in range(4):
        f = ALPHA if u < 3 else 1.0
        ts(XA[u], lnG(u), f, ALU.mult, reads=["lnT"], writes=[("XA", u)])
        ts(XB[u], lnB(u), f, ALU.mult, reads=["lnT"], writes=[("XB", u)])

    def do_ada(u):
        bank = next_bank()
        for g in range(24):
            b = ws_use()
            for jj in range(2):
                chunk = 2 * g + jj
                for kc in range(KC):
                    last = (kc == KC - 1)
                    mm(ps[bank][:, chunk * 2:chunk * 2 + 2],
                       wbuf[b][:, (jj * 16 + kc) * 128:(jj * 16 + kc + 1) * 128],
                       scT[:, kc * 2:kc * 2 + 2], kc == 0, last,
                       reads=[("wbuf", b), "scT"], writes=[("ps", bank)], signal=last)
            ws_load()
        for j in range(2):
            o = (u * 2 + j) * 48
            src = ps[bank][:, 0:96].rearrange("p (c j) -> p c j", j=2)[:, :, j]
            tt(DVE, MOD[:, o:o + 48], src, badaT[:, u * 48:(u + 1) * 48], ALU.add,
               reads=[("ps", bank), "badaT"], writes=[("MOD", u, j)])
            ts(OPS[(u, j)], modv(u, j, 1), 1.0, ALU.add, reads=[("MOD", u, j)], writes=[("OPS", u, j)])
            if u == 0:
                ts(A0[j], OPS[(0, j)], 1.0 / ALPHA, ALU.mult, reads=[("OPS", 0, j)], writes=[("A0", j)])
            else:
                tt(DVE, HA[(u, j)], lnG(u - 1), OPS[(u, j)], ALU.mult, reads=["lnT", ("OPS", u, j)], writes=[("HA", u, j)])
                tt(DVE, HB[(u, j)], lnB(u - 1), OPS[(u, j)], ALU.mult, reads=["lnT", ("OPS", u, j)], writes=[("HB", u, j)])
                tt(DVE, HB[(u, j)], HB[(u, j)], modv(u, j, 0), ALU.add, reads=[("HB", u, j), ("MOD", u, j)],
                   writes=[("HB", u, j)])

    tcopy(DVE, identb[:, :], ident[:, :], reads=["ident"], writes=["identb"])
    tcopy(DVE, permTb[:, :], permT[:, :], reads=["permT"], writes=["permTb"])
    for h in range(16):
        gi = h % 2
        cx.dma(SP, c_Gf, out=Gf[:, :], in_=G_d[h], writes=["Gf"])
        act(Ghl[gi][:, 0:320], Gf[:, :], AF.Identity, reads=["Gf"], writes=[("G", gi)])
        w = next_wk()
        tt(DVE, wk[w][:, 0:320], Gf[:, :], Ghl[gi][:, 0:320], ALU.subtract, reads=["Gf", ("G", gi)], writes=[("wk", w)])
        act(Ghl[gi][:, 320:640], wk[w][:, 0:320], AF.Identity, reads=[("wk", w)], writes=[("G", gi)])
        cx.dma(SP, c_Gs[gi], out=Gscr[h], in_=Ghl[gi][:, :], reads=[("G", gi)], writes=[("Gscr", h)])

    do_ada(0)

    def xa_c(k, T):
        return xa[:, k * TT:k * TT + T]

    def r1_c(k, T):
        return R1[:, k * TT:k * TT + T]

    def q_c(k, T):
        return R2[:, k * TT:k * TT + T]

    def stg(j):
        return R2f[:, j * 2048:(j + 1) * 2048]

    def stg_keys(j):
        return [("q", k) for k in range(8 * j, 8 * j + 8)]

    def hid_c(hc, T):
        if hc < 16:
            return KA[:, 8192 + hc * TT:8192 + hc * TT + T]
        return VA[:, 8192 + (hc - 16) * TT:8192 + (hc - 16) * TT + T]

    def hid_key(hc):
        if hc < 16:
            return ("KA", 1, hc)
        return ("VA", 1, (hc - 16) // 4)

    def proj_stat(ngroups, rhs_fn, rhs_keys, T, evac, kcs=16, chunks_per_group=2):
        for g in range(ngroups):
            b = ws_use()
            for jj in range(chunks_per_group):
                chunk = chunks_per_group * g + jj
                bank = next_bank()
                for kc in range(kcs):
                    last = (kc == kcs - 1)
                    mm(ps[bank][:, 0:T], wbuf[b][:, (jj * kcs + kc) * 128:(jj * kcs + kc + 1) * 128], rhs_fn(kc),
                       kc == 0, last, reads=[("wbuf", b), rhs_keys(kc)], writes=[("ps", bank)], signal=last)
                evac(chunk, bank)
            ws_load()

    def proj_mov(ngroups, lhs_fn, lhs_keys, ntb, tbw, evac):
        for g in range(ngroups):
            b = ws_use()
            for tb in range(ntb):
                bank = next_bank()
                for kc in range(KC):
                    last = (kc == KC - 1)
                    mm(ps[bank][0:tbw, 0:256], lhs_fn(kc, tb), wbuf[b][:, kc * 256:(kc + 1) * 256],
                       kc == 0, last, reads=[("wbuf", b), lhs_keys(kc)], writes=[("ps", bank)], signal=last)
                evac(g, tb, bank)
            ws_load()

    def out_small(dst_ap, src_ap, wki):
        cx.dma(SP, c_wkout[wki], out=dst_ap, in_=src_ap, reads=[("wk", wki)], writes=[])

    def resid_and_stats(n, bank, T, gate_ap, j, u, do_stats):
        stt(xa_c(n, T), ps[bank][:, 0:T], gate_ap[:, n:n + 1], xa_c(n, T), ALU.mult, ALU.add,
            reads=[("ps", bank), ("xa", n), ("MOD", u, j)], writes=[("xa", n)])
        if do_stats:
            if n == 0:
                act(s2[:, 0:T], xa_c(n, T), AF.Square, reads=[("xa", n)], writes=["s2"])
                tcopy(DVE, s1[:, 0:T], xa_c(n, T), reads=[("xa", n)], writes=["s1"])
            else:
                w = next_wk()
                act(wk[w][:, 0:T], xa_c(n, T), AF.Square, reads=[("xa", n)], writes=[("wk", w)])
                tt(PL[0], s1[:, 0:T], s1[:, 0:T], xa_c(n, T), ALU.add, reads=["s1", ("xa", n)], writes=["s1"])
                tt(PL[0], s2[:, 0:T], s2[:, 0:T], wk[w][:, 0:T], ALU.add, reads=["s2", ("wk", w)], writes=["s2"])

    def ln_finish(T, u, j, final):
        b1 = next_bank()
        mm(ps[b1][:, 0:T], onesD[:, :], s1[:, 0:T], True, True, reads=["onesD", "s1"], writes=[("ps", b1)], signal=True)
        b2 = next_bank()
        mm(ps[b2][:, 0:T], onesD[:, :], s2[:, 0:T], True, True, reads=["onesD", "s2"], writes=[("ps", b2)], signal=True)
        tcopy(DVE, msb[:, 0:T], ps[b1][:, 0:T], reads=[("ps", b1)], writes=["msb"])
        tt(DVE, s1[:, 0:T], msb[:, 0:T], msb[:, 0:T], ALU.mult, reads=["msb"], writes=["s1"])
        tt(DVE, s1[:, 0:T], ps[b2][:, 0:T], s1[:, 0:T], ALU.subtract, reads=[("ps", b2), "s1"], writes=["s1"])
        act(s2[:, 0:T], s1[:, 0:T], AF.Ln, reads=["s1", "epsT"], writes=["s2"], bias=epsT[:, 0:1])
        act(rstd[:, 0:T], s2[:, 0:T], AF.Exp, reads=["s2"], writes=["rstd"], scale=-0.5)
        stt(nmr[:, 0:T], msb[:, 0:T], -1.0, rstd[:, 0:T], ALU.mult, ALU.mult, reads=["msb", "rstd"], writes=["nmr"])
        for k in range(KC):
            w = next_wk()
            tt(DVE, wk[w][:, 0:T], xa_c(k, T), rstd[:, 0:T], ALU.mult, reads=[("xa", k), "rstd"], writes=[("wk", w)])
            tt(PL[0], wk[w][:, 0:T], wk[w][:, 0:T], nmr[:, 0:T], ALU.add, reads=[("wk", w), "nmr"], writes=[("wk", w)])
            act(xa_c(k, T), wk[w][:, 0:T], AF.Identity, reads=[("wk", w), ("XA", u), ("XB", u)], writes=[("xa", k)],
                scale=XA[u][:, k:k + 1], bias=XB[u][:, k:k + 1])
            if not final:
                act(r1_c(k, T), wk[w][:, 0:T], AF.Identity, reads=[("wk", w), ("HA", u + 1, j), ("HB", u + 1, j)],
                    writes=[("R1", k)], scale=HA[(u + 1, j)][:, k:k + 1], bias=HB[(u + 1, j)][:, k:k + 1])

    def mlp(T, u, j, final):
        gate_ap = modv(u, j, 2)
        for half in range(2):
            def evac_up(chunk, bank):
                w = next_wk()
                act(wk[w][:, 0:T], ps[bank][:, 0:T], AF.Relu, reads=[("ps", bank)], writes=[("wk", w)])
                tt(DVE, hid_c(chunk, T), wk[w][:, 0:T], wk[w][:, 0:T], ALU.mult, reads=[("wk", w)],
                   writes=[hid_key(chunk)])
            proj_stat(16, lambda kc: r1_c(kc, T), lambda kc: ("R1", kc), T, evac_up)

            def evac_dn(n, bank):
                resid_and_stats(n, bank, T, gate_ap, j, u, do_stats=(half == 1))
            proj_stat(16, lambda kc: hid_c(kc, T), lambda kc: hid_key(kc), T, evac_dn, kcs=32, chunks_per_group=1)
        ln_finish(T, u, j, final)

    def oproj(T, u, j):
        gate_ap = modv(u, j, 2)

        def evac(n, bank):
            resid_and_stats(n, bank, T, gate_ap, j, u, do_stats=True)
        proj_stat(8, lambda kc: r1_c(kc, T), lambda kc: ("R1", kc), T, evac)
        ln_finish(T, u, j, False)

    def out_transposed(src_fn, src_keys, nfc, tbw, dst_fn, width_cols=128):
        for fc0 in range(0, nfc, 4):
            nf = min(4, nfc - fc0)
            bank = next_bank()
            for i in range(nf):
                cx.op(PE, lambda i=i: nc.tensor.transpose(ps[bank][0:tbw, i * 128:(i + 1) * 128], src_fn(fc0 + i), ident[:, :]),
                      reads=[src_keys(fc0 + i), "ident"], writes=[("ps", bank)], signal=(i == nf - 1))
            w = next_wk()
            act(wk[w][0:tbw, 0:nf * 128], ps[bank][0:tbw, 0:nf * 128], AF.Identity, reads=[("ps", bank)], writes=[("wk", w)])
            dst_fn(fc0, nf, w)

    PL = [DVE]
    PREF = set()
    def run_pass(pi):
        PASS_MARK.append((pi, getattr(PE, 'nops', 0)))
        is_sample = (pi == 8)
        PL[0] = DVE if pi == 0 else POOL
        j = 1 if is_sample else 0
        T = 16 if is_sample else TT
        ntb = 1 if is_sample else 4
        tbw = 16 if is_sample else 128
        nqc = 1 if is_sample else 8
        qcw = 16 if is_sample else 64
        has_hist = is_sample or pi > 0
        is_last = is_sample or pi == 7
        xsrc = xs if is_sample else xp
        row0 = 0 if is_sample else pi * TT

        pre = pi in PREF
        if not pre:
            cx.dma(SP, c_ropeC, out=ropeC[:, :], in_=ropeC_d[pi], writes=["ropeC"])
            cx.dma(SP, c_ropeS, out=ropeS[:, :], in_=ropeS_d[pi], writes=["ropeS"])

        if is_sample:
            for blk in range(4):
                cx.dma(POOL, c_cache, out=VA[:, blk * 2048:(blk + 1) * 2048], in_=cav[blk * 128:(blk + 1) * 128, :],
                       writes=[("VA", 0, blk)])
            cx.dma(POOL, c_cache, out=VB[:, 0:512], in_=cbv[:, :], writes=[("VB", 0)])
            for k_ in [("VA", 0, 0), ("VA", 0, 1), ("VA", 0, 2), ("VA", 0, 3), ("VB", 0)]:
                cx.st[k_] = [(c_cache, c_cache.cnt), []]
            for blk in range(4):
                s = blk % 2
                cx.dma(SP, c_stg[s], out=stg(s), in_=cak[blk * 128:(blk + 1) * 128, :], writes=stg_keys(s))
                for h0 in range(0, 16, 4):
                    bank = next_bank()
                    for i in range(4):
                        h = h0 + i
                        cx.op(PE, lambda i=i, h=h, s=s: nc.tensor.transpose(ps[bank][:, i * 128:(i + 1) * 128],
                                                                             stg(s)[:, h * 128:(h + 1) * 128], ident[:, :]),
                              reads=stg_keys(s) + ["ident"], writes=[("ps", bank)], signal=(i == 3))
                    dst = KA[:, 0:8192].rearrange("p (h t) -> p h t", t=TT)[:, h0:h0 + 4, blk * 128:(blk + 1) * 128]
                    src = ps[bank][:, :].rearrange("p (h t) -> p h t", t=128)
                    cx.op(ACT, lambda dst=dst, src=src: nc.scalar.activation(out=dst, in_=src, func=AF.Identity),
                          reads=[("ps", bank)], writes=[("KA", 0, h0 + i) for i in range(4)])
            s = 0
            stgv = stg(s)[:, 0:1024].rearrange("p (g r d) -> p g r d", r=2, d=64)
            srcv = cbk[:, :].rearrange("p (g d) -> p g d", d=64)
            for r in range(2):
                cx.dma(SP, c_stg[s], out=stgv[:, :, r, :], in_=srcv, writes=stg_keys(s))
            for g0 in range(0, 8, 4):
                bank = next_bank()
                for i in range(4):
                    g = g0 + i
                    cx.op(PE, lambda i=i, g=g: nc.tensor.transpose(ps[bank][:, i * 128:(i + 1) * 128],
                                                                    stg(0)[:, g * 128:(g + 1) * 128], ident[:, :]),
                          reads=stg_keys(0) + ["ident"], writes=[("ps", bank)], signal=(i == 3))
                dst = KB[:, :].rearrange("p (g t) -> p g t", t=640)[:, g0:g0 + 4, 0:128]
                src = ps[bank][:, :].rearrange("p (g t) -> p g t", t=128)
                cx.op(ACT, lambda dst=dst, src=src: nc.scalar.activation(out=dst, in_=src, func=AF.Identity),
                      reads=[("ps", bank)], writes=[("KB", g0 + i) for i in range(4)])

        for tb in range(ntb):
            s = tb % 2
            if not (pre and tb < 2):
                cx.dma(SP, c_stg[s], out=stg(s)[0:tbw, :], in_=xsrc[row0 + tb * tbw:row0 + (tb + 1) * tbw, :],
                       writes=stg_keys(s))
            for k0 in range(0, KC, 4):
                bank = next_bank()
                for i in range(4):
                    k = k0 + i
                    cx.op(PE, lambda i=i, k=k, s=s: nc.tensor.transpose(ps[bank][:, i * 128:i * 128 + tbw],
                                                                         stg(s)[0:tbw, k * 128:(k + 1) * 128],
                                                                         ident[0:tbw, 0:tbw]),
                          reads=stg_keys(s) + ["ident"], writes=[("ps", bank)], signal=(i == 3))
                for i in range(4):
                    k = k0 + i
                    src = ps[bank][:, i * 128:i * 128 + tbw]
                    dsl = slice(k * TT + tb * tbw, k * TT + (tb + 1) * tbw)
                    cx.op(DVE, lambda dsl=dsl, src=src: nc.vector.tensor_scalar(
                        out=xa[:, dsl], in0=src, scalar1=ALPHA, scalar2=None, op0=ALU.mult),
                        reads=[("ps", bank)], writes=[("xa", k)])
                    cx.op(ACT, lambda k=k, dsl=dsl: nc.scalar.activation(
                        out=R1[:, dsl], in_=xa[:, dsl], func=AF.Identity,
                        scale=A0[j][:, k:k + 1], bias=modv(0, j, 0)[:, k:k + 1]),
                        reads=[("xa", k), ("A0", j), ("MOD", 0, j)], writes=[("R1", k)])

        ckpt(4)
        scaleA = float(128 ** -0.5)

        def evac_qa(h, bank):
            act(q_c(h, T), ps[bank][:, 0:T], AF.Identity, reads=[("ps", bank)], writes=[("q", h)], scale=scaleA)
        proj_stat(8, lambda kc: r1_c(kc, T), lambda kc: ("R1", kc), T, evac_qa)

        def evac_ka(h, bank):
            tcopy(DVE, KA[:, 8192 + h * TT:8192 + h * TT + T], ps[bank][:, 0:T], reads=[("ps", bank)],
                  writes=[("KA", 1, h)])
            if is_last:
                w = next_wk()
                tcopy(DVE, wk[w][:, 0:T], ps[bank][:, 0:T], reads=[("ps", bank)], writes=[("wk", w)])
                dsto = ak_s if is_sample else ak_p
                bk2 = next_bank()
                for tb in range(ntb):
                    cx.op(PE, lambda tb=tb, w=w: nc.tensor.transpose(ps[bk2][0:tbw, tb * 128:(tb + 1) * 128],
                                                                      wk[w][:, tb * tbw:(tb + 1) * tbw], ident[:, :]),
                          reads=[("wk", w), "ident"], writes=[("ps", bk2)], signal=(tb == ntb - 1))
                w2 = next_wk()
                act(wk[w2][0:tbw, 0:ntb * 128], ps[bk2][0:tbw, 0:ntb * 128], AF.Identity, reads=[("ps", bk2)],
                    writes=[("wk", w2)])
                dst = dsto[:, h * 128:(h + 1) * 128].rearrange("(tb p) d -> p tb d", p=tbw)
                src = wk[w2][0:tbw, 0:ntb * 128].rearrange("p (tb d) -> p tb d", d=128)
                out_small(dst, src, w2)
        proj_stat(8, lambda kc: r1_c(kc, T), lambda kc: ("R1", kc), T, evac_ka)

        def evac_va(g, tb, bank):
            tcopy(DVE, VA[0:tbw, 8192 + tb * 2048 + g * 256:8192 + tb * 2048 + (g + 1) * 256], ps[bank][0:tbw, 0:256],
                  reads=[("ps", bank)], writes=[("VA", 1, tb)])
            if is_last:
                w = next_wk()
                tcopy(DVE, wk[w][0:tbw, 0:256], ps[bank][0:tbw, 0:256], reads=[("ps", bank)], writes=[("wk", w)])
                dsto = av_s if is_sample else av_p
                out_small(dsto[tb * tbw:(tb + 1) * tbw, g * 256:(g + 1) * 256], wk[w][0:tbw, 0:256], w)
        proj_mov(8, lambda kc, tb: R1[:, kc * TT + tb * tbw:kc * TT + (tb + 1) * tbw], lambda kc: ("R1", kc), ntb, tbw,
                 evac_va)

        if pi == 0:
            do_ada(1)
        ckpt(5)
        kbs = []
        if has_hist:
            for i in range(4):
                kbs.append((0, i, 128, 2 * i, True))
        for i in range(ntb):
            kbs.append((1, i, tbw, 8 + 2 * i, not is_sample))
        pend = []

        def flush_pend():
            while pend:
                pend.pop(0)()

        for h in range(16):
            gi = h % 2
            cx.dma(SP, c_G[gi], out=Ghl[gi][:, :], in_=Gscr[h], reads=[("Gscr", h)], writes=[("G", gi)])
            pvb = 4 + 2 * (h % 2)
            dnb = pvb + 1
            mm(ps[pvb][:, 0:T], zb[:, 0:128], zb[:, 0:T], True, False, reads=["zb"], writes=[("ps", pvb)], signal=False)
            mm(ps[dnb][:, 0:T], zb[:, 0:128], zb[:, 0:T], True, False, reads=["zb"], writes=[("ps", dnb)], signal=False)
            nk_list = []
            for (slot, blk, nk, e0, two) in kbs:
                c_lo = max(0, e0 - 8)
                c_hi = min(nqc - 1, e0 + 1 if two else e0)
                if c_hi < c_lo:
                    continue
                nk_list.append((slot, blk, nk, e0, two, c_lo, c_hi))
            for idx, (slot, blk, nk, e0, two, c_lo, c_hi) in enumerate(nk_list):
                lastkb = (idx == len(nk_list) - 1)
                ncols = (c_hi - c_lo + 1) * qcw
                q0 = c_lo * qcw
                sb_ = next_bank()
                m0 = 64 * (9 - e0) + q0
                nn = max(0, min(ncols, 320 - m0))
                mm(ps[sb_][0:nk, 0:ncols], KA[:, slot * 8192 + h * TT + blk * 128:slot * 8192 + h * TT + blk * 128 + nk],
                   R2[:, h * TT + q0:h * TT + q0 + ncols], True, nn == 0,
                   reads=[("KA", slot, h), ("q", h)], writes=[("ps", sb_)], signal=(nn == 0))
                if nn > 0:
                    mm(ps[sb_][0:nk, 0:nn], identb[0:nk, 0:nk], Ghl[gi][0:nk, m0:m0 + nn], False, False,
                       reads=["identb", ("G", gi)], writes=[("ps", sb_)], signal=False)
                    mm(ps[sb_][0:nk, 0:nn], identb[0:nk, 0:nk], Ghl[gi][0:nk, 320 + m0:320 + m0 + nn], False, True,
                       reads=["identb", ("G", gi)], writes=[("ps", sb_)], signal=True)
                p = next_pt()
                if nn > 0:
                    act(PT[p][0:nk, 0:nn], ps[sb_][0:nk, 0:nn], AF.Exp, reads=[("ps", sb_)], writes=[("PT", p)])
                if nn < ncols:
                    act(PT[p][0:nk, nn:ncols], ps[sb_][0:nk, nn:ncols], AF.Exp, reads=[("ps", sb_), "tabc"],
                        writes=[("PT", p)], bias=tabc[0:nk, h:h + 1])
                if two:
                    c = e0 + 1
                    if c_lo <= c <= c_hi:
                        cx.op(PL[0], lambda p=p, c=c: PL[0].e.memset(PT[p][0:64, (c - c_lo) * qcw:(c - c_lo + 1) * qcw], 0.0),
                              writes=[("PT", p)])
                    c = e0 - 8
                    if c_lo <= c <= c_hi:
                        cx.op(PL[0], lambda p=p, c=c: PL[0].e.memset(PT[p][64:128, (c - c_lo) * qcw:(c - c_lo + 1) * qcw], 0.0),
                              writes=[("PT", p)])

                def pv(slot=slot, blk=blk, nk=nk, p=p, q0=q0, ncols=ncols, lastkb=lastkb, pvb=pvb, dnb=dnb, h=h):
                    mm(ps[pvb][:, q0:q0 + ncols], VA[0:nk, slot * 8192 + blk * 2048 + h * 128:slot * 8192 + blk * 2048 + (h + 1) * 128],
                       PT[p][0:nk, 0:ncols], False, lastkb, reads=[("VA", slot, blk), ("PT", p)], writes=[("ps", pvb)],
                       signal=lastkb)
                    mm(ps[dnb][:, q0:q0 + ncols], onesb[0:nk, :], PT[p][0:nk, 0:ncols], False, lastkb,
                       reads=["onesb", ("PT", p)], writes=[("ps", dnb)], signal=lastkb)
                    if lastkb:
                        wl = next_wk()
                        act(wk[wl][:, 0:T], ps[dnb][:, 0:T], AF.Ln, reads=[("ps", dnb)], writes=[("wk", wl)])
                        act(rden[:, 0:T], wk[wl][:, 0:T], AF.Exp, reads=[("wk", wl)], writes=["nmr"], scale=-1.0)
                        tt(DVE, r1_c(h, T), ps[pvb][:, 0:T], rden[:, 0:T], ALU.mult, reads=[("ps", pvb), "nmr"],
                           writes=[("R1", h)])
                pend.append(pv)
                if len(pend) > min(LOOK, len(kbs) - 1):
                    pend.pop(0)()
        flush_pend()

        if not is_sample and pi < NT - 1:
            for h in range(16):
                pass
            cx.op(PL[0], lambda: PL[0].e.tensor_copy(out=KA[:, 0:8192], in_=KA[:, 8192:16384]),
                  reads=[("KA", 1, h) for h in range(16)], writes=[("KA", 0, h) for h in range(16)])
            cx.op(PL[0], lambda: PL[0].e.tensor_copy(out=VA[:, 0:8192], in_=VA[:, 8192:16384]),
                  reads=[("VA", 1, b) for b in range(4)], writes=[("VA", 0, b) for b in range(4)])

        ckpt(6)
        oproj(T, 0, j)
        if pi == 0:
            do_ada(2)
        ckpt(7)
        mlp(T, 1, j, False)
        ckpt(8)

        def rope_evac(dst_ap, dst_key, bank, qscale, keep_f32=None):
            w0 = next_wk()
            act(wk[w0][:, 0:T], ps[bank][:, 0:T], AF.Identity, reads=[("ps", bank)], writes=[("wk", w0)], scale=qscale)
            ph = next_pt()
            act(PT[ph][:, 0:T], ps[bank][:, 0:T], AF.Identity, reads=[("ps", bank)], writes=[("PT", ph)], scale=qscale)
            b2 = next_bank()
            mm(ps[b2][:, 0:T], permTb[:, :], PT[ph][:, 0:T], True, True, reads=["permTb", ("PT", ph)], writes=[("ps", b2)],
               signal=True)
            w1 = next_wk()
            tt(DVE, wk[w1][:, 0:T], ps[b2][:, 0:T], ropeS[:, 0:T], ALU.mult, reads=[("ps", b2), "ropeS"], writes=[("wk", w1)])
            tt(PL[0], wk[w0][:, 0:T], wk[w0][:, 0:T], ropeC[:, 0:T], ALU.mult, reads=[("wk", w0), "ropeC"], writes=[("wk", w0)])
            if keep_f32 is None:
                tt(DVE, dst_ap, wk[w0][:, 0:T], wk[w1][:, 0:T], ALU.add, reads=[("wk", w0), ("wk", w1)], writes=[dst_key])
            else:
                tt(DVE, wk[w0][:, 0:T], wk[w0][:, 0:T], wk[w1][:, 0:T], ALU.add, reads=[("wk", w0), ("wk", w1)],
                   writes=[("wk", w0)])
                tcopy(PL[0], dst_ap, wk[w0][:, 0:T], reads=[("wk", w0)], writes=[dst_key])
                keep_f32(w0)

        def evac_qb(i, bank):
            rope_evac(q_c(i, T), ("q", i), bank, 0.125)
        proj_stat(8, lambda kc: r1_c(kc, T), lambda kc: ("R1", kc), T, evac_qb)

        def evac_kb(g, bank):
            dst = KB[:, g * 640 + 128:g * 640 + 128 + T]
            if is_last:
                def keep(w0):
                    t0 = T - tbw
                    bk2 = next_bank()
                    cx.op(PE, lambda: nc.tensor.transpose(ps[bk2][0:tbw, 0:128], wk[w0][:, t0:t0 + tbw], ident[:, :]),
                          reads=[("wk", w0), "ident"], writes=[("ps", bk2)], signal=True)
                    w2 = next_wk()
                    act(wk[w2][0:tbw, 0:64], ps[bk2][0:tbw, 0:64], AF.Identity, reads=[("ps", bk2)], writes=[("wk", w2)])
                    dsto = bk_s if is_sample else bk_p
                    out_small(dsto[:, g * 64:(g + 1) * 64], wk[w2][0:tbw, 0:64], w2)
                rope_evac(dst, ("KB", g), bank, 1.0, keep_f32=keep)
            else:
                rope_evac(dst, ("KB", g), bank, 1.0)
        proj_stat(4, lambda kc: r1_c(kc, T), lambda kc: ("R1", kc), T, evac_kb)

        def evac_vb(g, tb, bank):
            tcopy(DVE, VB[0:tbw, (1 + tb) * 512 + g * 256:(1 + tb) * 512 + (g + 1) * 256], ps[bank][0:tbw, 0:256],
                  reads=[("ps", bank)], writes=[("VB", 1 + tb)])
            if is_last and tb == ntb - 1:
                w = next_wk()
                tcopy(DVE, wk[w][0:tbw, 0:256], ps[bank][0:tbw, 0:256], reads=[("ps", bank)], writes=[("wk", w)])
                dsto = bv_s if is_sample else bv_p
                out_small(dsto[:, g * 256:(g + 1) * 256], wk[w][0:tbw, 0:256], w)
        proj_mov(2, lambda kc, tb: R1[:, kc * TT + tb * tbw:kc * TT + (tb + 1) * tbw], lambda kc: ("R1", kc), ntb, tbw,
                 evac_vb)

        if pi == 0:
            do_ada(3)
        ckpt(9)
        kbsB = []
        if has_hist:
            kbsB.append((0, 0, 128, 0, True))
        for i in range(ntb):
            kbsB.append((128 + i * 128, 1 + i, tbw, 2 + 2 * i, not is_sample))
        for pr in range(16):
            pvb = 4 + 2 * (pr % 2)
            dnb = pvb + 1
            g = pr // 2
            mm(ps[pvb][:, 0:T], zb[:, 0:128], zb[:, 0:T], True, False, reads=["zb"], writes=[("ps", pvb)], signal=False)
            mm(ps[dnb][:, 0:T], zb[:, 0:128], zb[:, 0:T], True, False, reads=["zb"], writes=[("ps", dnb)], signal=False)
            items = []
            for hf in range(2):
                for (col0, vblk, nk, e0, two) in kbsB:
                    c_lo = max(0, e0 - 2)
                    c_hi = min(nqc - 1, e0 + 1 if two else e0)
                    if c_hi < c_lo:
                        continue
                    items.append((hf, col0, vblk, nk, e0, two, c_lo, c_hi))
            for idx, (hf, col0, vblk, nk, e0, two, c_lo, c_hi) in enumerate(items):
                lastit = (idx == len(items) - 1)
                lasthalf = lastit or (items[idx + 1][0] != hf)
                ncols = (c_hi - c_lo + 1) * qcw
                q0 = c_lo * qcw
                p0 = 64 * hf
                sb_ = next_bank()
                mm(ps[sb_][0:nk, 0:ncols], KB[p0:p0 + 64, g * 640 + col0:g * 640 + col0 + nk],
                   R2[p0:p0 + 64, pr * TT + q0:pr * TT + q0 + ncols], True, True,
                   reads=[("KB", g), ("q", pr)], writes=[("ps", sb_)], signal=True)
                p = next_pt()
                act(PT[p][0:nk, 0:ncols], ps[sb_][0:nk, 0:ncols], AF.Exp, reads=[("ps", sb_)], writes=[("PT", p)])
                if two:
                    c = e0 + 1
                    if c_lo <= c <= c_hi:
                        cx.op(PL[0], lambda p=p, c=c, c_lo=c_lo: PL[0].e.memset(
                            PT[p][0:64, (c - c_lo) * qcw:(c - c_lo + 1) * qcw], 0.0), writes=[("PT", p)])
                    c = e0 - 2
                    if c_lo <= c <= c_hi:
                        cx.op(PL[0], lambda p=p, c=c, c_lo=c_lo: PL[0].e.memset(
                            PT[p][64:128, (c - c_lo) * qcw:(c - c_lo + 1) * qcw], 0.0), writes=[("PT", p)])

                def pvB(vblk=vblk, nk=nk, p=p, q0=q0, ncols=ncols, lastit=lastit, pvb=pvb, dnb=dnb, g=g, p0=p0, pr=pr,
                        lasthalf=lasthalf):
                    mm(ps[pvb][p0:p0 + 64, q0:q0 + ncols], VB[0:nk, vblk * 512 + g * 64:vblk * 512 + (g + 1) * 64],
                       PT[p][0:nk, 0:ncols], False, lasthalf, reads=[("VB", vblk), ("PT", p)], writes=[("ps", pvb)],
                       signal=lastit)
                    mm(ps[dnb][p0:p0 + 64, q0:q0 + ncols], onesb[0:nk, 0:64], PT[p][0:nk, 0:ncols], False, lasthalf,
                       reads=["onesb", ("PT", p)], writes=[("ps", dnb)], signal=lastit)
                    if lastit:
                        wl = next_wk()
                        act(wk[wl][:, 0:T], ps[dnb][:, 0:T], AF.Ln, reads=[("ps", dnb), "esink"], writes=[("wk", wl)],
                            bias=esink[:, pr:pr + 1])
                        act(rden[:, 0:T], wk[wl][:, 0:T], AF.Exp, reads=[("wk", wl)], writes=["nmr"], scale=-1.0)
                        tt(DVE, r1_c(pr, T), ps[pvb][:, 0:T], rden[:, 0:T], ALU.mult, reads=[("ps", pvb), "nmr"],
                           writes=[("R1", pr)])
                pend.append(pvB)
                if len(pend) > min(LOOK, 2 * len(kbsB) - 1):
                    pend.pop(0)()
        flush_pend()

        if not is_sample and pi < NT - 1:
            cx.dma(SP, c_ropeC, out=ropeC[:, :], in_=ropeC_d[pi + 1], writes=["ropeC"])
            cx.dma(SP, c_ropeS, out=ropeS[:, :], in_=ropeS_d[pi + 1], writes=["ropeS"])
            for tb_ in range(2):
                cx.dma(SP, c_stg[tb_], out=stg(tb_)[0:128, :],
                       in_=xp[(pi + 1) * TT + tb_ * 128:(pi + 1) * TT + (tb_ + 1) * 128, :], writes=stg_keys(tb_))
            PREF.add(pi + 1)
        if not is_sample and pi < NT - 1:
            kbv = KB[:, :].rearrange("p (g t) -> p g t", t=640)
            cx.op(PL[0], lambda: PL[0].e.tensor_copy(out=kbv[:, :, 0:128], in_=kbv[:, :, 512:640]),
                  reads=[("KB", g) for g in range(8)], writes=[("KB", g) for g in range(8)])
            cx.op(PL[0], lambda: PL[0].e.tensor_copy(out=VB[:, 0:512], in_=VB[:, 2048:2560]),
                  reads=[("VB", 4)], writes=[("VB", 0)])

        ckpt(10)
        oproj(T, 2, j)
        mlp(T, 3, j, True)
        ckpt(11)

        ydst = y_s if is_sample else y_p
        for tb in range(ntb):
            def dstf(fc0, nf, w, tb=tb):
                out_small(ydst[row0 + tb * tbw:row0 + (tb + 1) * tbw, fc0 * 128:(fc0 + nf) * 128], wk[w][0:tbw, 0:nf * 128], w)
            out_transposed(lambda fc, tb=tb: xa[:, fc * TT + tb * tbw:fc * TT + (tb + 1) * tbw], lambda fc: ("xa", fc),
                           16, tbw, dstf)

    try:
        ckpt(3)
        for pi in range(NT):
            run_pass(pi)
        if DO_SAMPLE:
            run_pass(8)
    except StopBuild:
        pass

    for ch in c_wkout:
        if ch.cnt:
            nc.sync.wait_ge(ch.sem, ch.cnt)
    return nc


_CACHE = {}


def kernel(x_prompt, x_sample, cache_a_k, cache_a_v, cache_b_k, cache_b_v, c_prompt, c_sample,
           w_ada, b_ada, ln_g, ln_b, w_qkv_a, w_o_a, rel_bias_a, w_qkv_b, w_o_b, sink_b, w_up, w_down,
           _NT=8, _DO_SAMPLE=True, _cores=8):
    f = lambda a: np.ascontiguousarray(np.asarray(a, dtype=np.float32))
    x_prompt, x_sample = f(x_prompt), f(x_sample)
    wall = build_weights(f(w_ada), f(w_qkv_a), f(w_o_a), f(w_qkv_b), f(w_o_b), f(w_up), f(w_down))
    ropeC, ropeS, permT = rope_tables()
    ident = np.eye(128, dtype=np.float32)
    lg = f(ln_g).reshape(4, 16, 128).transpose(2, 0, 1)
    lb = f(ln_b).reshape(4, 16, 128).transpose(2, 0, 1)
    lnT = np.ascontiguousarray(np.concatenate([lg.reshape(128, 64), lb.reshape(128, 64)], axis=1))
    badaT = np.ascontiguousarray(f(b_ada).reshape(4, 48, 128).transpose(2, 0, 1).reshape(128, 192))
    tab = f(rel_bias_a)[0]
    pp = np.arange(128)[:, None]
    mmi = np.arange(320)[None, :]
    idx = np.clip(mmi - pp - 64, -128, 128) + 128
    G = np.ascontiguousarray(tab[:, idx])
    tabc = np.ascontiguousarray(np.broadcast_to(tab[:, 256][None, :], (128, 16)))
    sk = f(sink_b)[0]
    sinkT = np.ascontiguousarray(np.stack([np.broadcast_to(sk[0::2][None], (64, 16)),
                                           np.broadcast_to(sk[1::2][None], (64, 16))], 0).reshape(128, 16))
    key = (_NT, _DO_SAMPLE)
    if key not in _CACHE:
        _CACHE[key] = build(_NT, _DO_SAMPLE)
    nc = _CACHE[key]
    in_maps = []
    for b in range(_cores):
        cc = np.stack([f(c_prompt)[b], f(c_sample)[b]], axis=-1)
        cT = np.ascontiguousarray(cc.reshape(16, 128, 2).transpose(1, 0, 2).reshape(128, 32))
        in_maps.append({
            "xp": x_prompt[b], "xs": x_sample[b], "cT": cT,
            "cak": f(cache_a_k)[0, b].reshape(512, 2048), "cav": f(cache_a_v)[0, b].reshape(512, 2048),
            "cbk": f(cache_b_k)[0, b].reshape(128, 512), "cbv": f(cache_b_v)[0, b].reshape(128, 512),
            "wall": wall, "lnT": lnT, "badaT": badaT, "G": G, "tabc": tabc, "sinkT": sinkT,
            "ident": ident, "permT": permT, "ropeC": ropeC, "ropeS": ropeS,
        })
    res = run_bass_kernel_spmd(nc, in_maps, core_ids=list(range(_cores)))
    R = res.results
    st = lambda n: np.stack([np.asarray(r[n], dtype=np.float32) for r in R], 0)
    y_p = st("y_p")
    y_s = st("y_s")
    ak_p = st("ak_p").reshape(_cores, 512, 16, 128)[None]
    av_p = st("av_p").reshape(_cores, 512, 16, 128)[None]
    bk_p = st("bk_p").reshape(_cores, 128, 8, 64)[None]
    bv_p = st("bv_p").reshape(_cores, 128, 8, 64)[None]
    ak_s = st("ak_s").reshape(_cores, 16, 16, 128)[None]
    av_s = st("av_s").reshape(_cores, 16, 16, 128)[None]
    bk_s = st("bk_s").reshape(_cores, 16, 8, 64)[None]
    bv_s = st("bv_s").reshape(_cores, 16, 8, 64)[None]
    return (y_p, y_s, ak_p, av_p, bk_p, bv_p, ak_s, av_s, bk_s, bv_s)
```

```python
import numpy as np
import concourse.bass as bass
import concourse.mybir as mybir
from concourse.bass_utils import run_bass_kernel_spmd

F32 = mybir.dt.float32
BF16 = mybir.dt.bfloat16
AF = mybir.ActivationFunctionType
ALU = mybir.AluOpType

D = 2048
KC = 16
TT = 512
ALPHA = float(4.0 ** 0.25)
EPS = 1e-5
NBUF = 3
NADA = 96
NMAIN = 182
NGRP = NADA + NMAIN
SG = 14
ROT = 16
THETA = 500000.0


class Eng:
    def __init__(self, e, name, sem, is_pe=False):
        self.e = e
        self.name = name
        self.sem = sem
        self.cnt = 0
        self.seen = {}
        self.is_pe = is_pe


class Chan:
    def __init__(self, sem, name):
        self.sem = sem
        self.cnt = 0
        self.name = name


PASS_MARK = []


class StopBuild(Exception):
    pass


import os as _os
_STOP = int(_os.environ.get("K_STOP", "99"))
_SUB = int(_os.environ.get("K_SUB", "99"))


def ckpt(n):
    if _STOP <= n:
        raise StopBuild()


class Ctx:
    def __init__(self, nc):
        self.nc = nc
        self.st = {}
        self.sems = []

    def sem(self, name):
        s = self.nc.alloc_semaphore(name)
        self.sems.append(s)
        return s

    def _wait(self, E, reads, writes):
        deps = {}
        for k in reads:
            st = self.st.get(k)
            if st and st[0] is not None:
                s, v = st[0]
                if deps.get(s, 0) < v:
                    deps[s] = v
        for k in writes:
            st = self.st.get(k)
            if st:
                if st[0] is not None:
                    s, v = st[0]
                    if deps.get(s, 0) < v:
                        deps[s] = v
                for (s, v) in st[1]:
                    if deps.get(s, 0) < v:
                        deps[s] = v
        for s, v in deps.items():
            if s is E and E.is_pe:
                continue
            if E.seen.get(s, 0) < v:
                E.e.wait_ge(s.sem, v)
                E.seen[s] = v

    def _upd(self, reads, writes, stamp):
        for k in reads:
            st = self.st.get(k)
            if st is None:
                self.st[k] = [None, [stamp]]
            else:
                st[1] = [x for x in st[1] if x[0] is not stamp[0]] + [stamp]
        for k in writes:
            self.st[k] = [stamp, []]

    def op(self, E, fn, reads=(), writes=(), signal=True):
        self._wait(E, reads, writes)
        ins = fn()
        E.nops = getattr(E, 'nops', 0) + 1
        if signal:
            ins.then_inc(E.sem, 1)
            E.cnt += 1
            stamp = (E, E.cnt)
        else:
            stamp = (E, E.cnt + 1)
        self._upd(reads, writes, stamp)
        return ins

    def dma(self, Q, chan, out, in_, reads=(), writes=(), **kw):
        self._wait(Q, reads, writes)
        ins = Q.e.dma_start(out=out, in_=in_, **kw)
        ins.then_inc(chan.sem, 16)
        chan.cnt += 16
        self._upd(reads, writes, (chan, chan.cnt))
        return ins


def stat_groups(W):
    K, N = W.shape
    kc = K // 128
    assert kc == 16 and N % 256 == 0
    G = N // 256
    a = W.reshape(kc, 128, G, 2, 128)
    a = a.transpose(2, 1, 3, 0, 4)
    return np.ascontiguousarray(a).reshape(G, 128, 4096)


def mov_groups(W):
    K, N = W.shape
    kc = K // 128
    G = N // 256
    a = W.reshape(kc, 128, G, 256)
    a = a.transpose(2, 1, 0, 3)
    return np.ascontiguousarray(a).reshape(G, 128, 4096)


def down_groups(W, half):
    Wh = W[half * 4096:(half + 1) * 4096]
    a = Wh.reshape(32, 128, 16, 128)
    a = a.transpose(2, 1, 0, 3)
    return np.ascontiguousarray(a).reshape(16, 128, 4096)


def build_weights(w_ada, w_qkv_a, w_o_a, w_qkv_b, w_o_b, w_up, w_down):
    gs = []
    for u in range(4):
        gs.append(stat_groups(w_ada[u // 2, u % 2]))
    qa = w_qkv_a[0]
    gs.append(stat_groups(qa[:, 0:2048]))
    gs.append(stat_groups(qa[:, 2048:4096]))
    gs.append(mov_groups(qa[:, 4096:6144]))
    gs.append(stat_groups(w_o_a[0]))
    for h in range(2):
        gs.append(stat_groups(w_up[0][:, h * 4096:(h + 1) * 4096]))
        gs.append(down_groups(w_down[0], h))
    qb = w_qkv_b[0]
    gs.append(stat_groups(qb[:, 0:2048]))
    kb = qb[:, 2048:2560].reshape(2048, 8, 1, 64)
    kbd = np.broadcast_to(kb, (2048, 8, 2, 64)).reshape(2048, 1024)
    gs.append(stat_groups(kbd))
    gs.append(mov_groups(qb[:, 2560:3072]))
    gs.append(stat_groups(w_o_b[0]))
    for h in range(2):
        gs.append(stat_groups(w_up[1][:, h * 4096:(h + 1) * 4096]))
        gs.append(down_groups(w_down[1], h))
    wall = np.concatenate(gs, axis=0)
    assert wall.shape == (NGRP, 128, 4096), wall.shape
    return wall


def rope_tables():
    half = ROT // 2
    freqs = (np.float32(THETA) ** (-np.arange(0, ROT, 2, dtype=np.float32) / np.float32(ROT))).astype(np.float32)
    C = np.ones((9, 128, TT), np.float32)
    S = np.zeros((9, 128, TT), np.float32)
    for t in range(9):
        pos0 = t * TT if t < 8 else 1024
        pos = (pos0 + np.arange(TT)).astype(np.float32)
        ang = (pos[:, None] * freqs[None, :]).astype(np.float32)
        cs = np.cos(ang).astype(np.float32).T
        sn = np.sin(ang).astype(np.float32).T
        for hb in (0, 64):
            C[t, hb:hb + 8] = cs
            C[t, hb + 8:hb + 16] = cs
            S[t, hb:hb + 8] = -sn
            S[t, hb + 8:hb + 16] = sn
    permT = np.zeros((128, 128), np.float32)
    for m in range(128):
        d = m % 64
        if d < 8:
            permT[m + 8, m] = 1.0
        elif d < 16:
            permT[m - 8, m] = 1.0
    return C, S, permT


def build(NT=8, DO_SAMPLE=True):
    nc = bass.Bass("TRN2", target_bir_lowering=False)
    cx = Ctx(nc)

    def din(name, shape, dt=F32):
        return nc.dram_tensor(name, list(shape), dt, kind="ExternalInput")

    def dout(name, shape):
        return nc.dram_tensor(name, list(shape), F32, kind="ExternalOutput")

    xp = din("xp", [4096, 2048])
    xs = din("xs", [16, 2048])
    cT_d = din("cT", [128, 32])
    cak = din("cak", [512, 2048])
    cav = din("cav", [512, 2048])
    cbk = din("cbk", [128, 512])
    cbv = din("cbv", [128, 512])
    wall = din("wall", [NGRP, 128, 4096])
    lnT_d = din("lnT", [128, 128])
    bada_d = din("badaT", [128, 192])
    G_d = din("G", [16, 128, 320])
    tabc_d = din("tabc", [128, 16])
    sink_d = din("sinkT", [128, 16])
    ident_d = din("ident", [128, 128])
    perm_d = din("permT", [128, 128])
    ropeC_d = din("ropeC", [9, 128, TT])
    ropeS_d = din("ropeS", [9, 128, TT])
    wscr_m = nc.dram_tensor("wscr_m", [NMAIN, 128, 4096], BF16, kind="Internal")
    Gscr = nc.dram_tensor("Gscr", [16, 128, 640], BF16, kind="Internal")

    def wscr_g(g):
        assert g >= NADA
        return wscr_m[g - NADA]

    y_p = dout("y_p", [4096, 2048])
    y_s = dout("y_s", [16, 2048])
    ak_p = dout("ak_p", [512, 2048])
    av_p = dout("av_p", [512, 2048])
    bk_p = dout("bk_p", [128, 512])
    bv_p = dout("bv_p", [128, 512])
    ak_s = dout("ak_s", [16, 2048])
    av_s = dout("av_s", [16, 2048])
    bk_s = dout("bk_s", [16, 512])
    bv_s = dout("bv_s", [16, 512])

    def sb(name, shape, dt):
        return nc.alloc_sbuf_tensor("sb_" + name, list(shape), dt)

    xa = sb("xa", [128, 8192], F32)
    R1 = sb("R1", [128, 8192], BF16)
    R2f = sb("R2f", [128, 4096], F32)
    R2 = R2f.bitcast(BF16)
    KA = sb("KA", [128, 16384], BF16)
    VA = sb("VA", [128, 16384], BF16)
    KB = sb("KB", [128, 8 * 640], BF16)
    VB = sb("VB", [128, 5 * 512], BF16)
    wbuf = [sb(f"wbuf{i}", [128, 4096], BF16) for i in range(NBUF)]
    ident = sb("ident", [128, 128], F32)
    permT = sb("permT", [128, 128], F32)
    onesD = sb("onesD", [128, 128], F32)
    onesb = sb("onesb", [128, 128], BF16)
    zb = sb("zb", [128, 512], BF16)
    ropeC = sb("ropeC", [128, TT], F32)
    ropeS = sb("ropeS", [128, TT], F32)
    s1 = sb("s1", [128, TT], F32)
    s2 = sb("s2", [128, TT], F32)
    msb = sb("msb", [128, TT], F32)
    rstd = sb("rstd", [128, TT], F32)
    nmr = sb("nmr", [128, TT], F32)
    rden = nmr
    NWK = 4
    wk = [sb(f"wk{i}", [128, TT], F32) for i in range(NWK)]
    NPT = 6
    LOOK = 5
    PT = [sb(f"PT{i}", [128, TT], BF16) for i in range(NPT)]
    Gf = sb("Gf", [128, 320], F32)
    Ghl = [sb(f"Ghl{i}", [128, 640], BF16) for i in range(2)]
    identb = sb("identb", [128, 128], BF16)
    permTb = sb("permTb", [128, 128], BF16)
    tabc = sb("tabc", [128, 16], F32)
    sinkT = sb("sinkT", [128, 16], F32)
    esink = sb("esink", [128, 16], F32)
    lnT = sb("lnT", [128, 128], F32)
    badaT = sb("badaT", [128, 192], F32)
    cT = sb("cT", [128, 32], F32)
    scT = sb("scT", [128, 32], BF16)
    sig = sb("sig", [128, 32], F32)
    MOD = sb("MOD", [128, 4 * 2 * 48], F32)
    SC = sb("SC", [128, 32 * 16], F32)
    epsT = sb("epsT", [128, 1], F32)

    ps = [nc.alloc_psum_tensor(f"ps{i}", [128, 512], F32) for i in range(8)]

    PE = Eng(nc.tensor, "pe", cx.sem("s_pe"), is_pe=True)
    ACT = Eng(nc.scalar, "act", cx.sem("s_act"))
    DVE = Eng(nc.vector, "dve", cx.sem("s_dve"))
    POOL = Eng(nc.gpsimd, "pool", cx.sem("s_pool"))
    SP = Eng(nc.sync, "sp", None)

    def chan(name):
        return Chan(cx.sem(name), name)

    wchan = [chan(f"c_w{i}") for i in range(NBUF)]
    c_const = chan("c_const")
    c_scr = [chan(f"c_scr{i}") for i in range(NBUF)]
    c_wsw = [chan(f"c_wsw{i}") for i in range(NBUF)]
    c_stg = [chan("c_stg0"), chan("c_stg1")]
    c_G = [chan("c_G0"), chan("c_G1")]
    c_Gf = chan("c_Gf")
    c_Gs = [chan("c_Gs0"), chan("c_Gs1")]
    c_ropeC = chan("c_ropeC")
    c_ropeS = chan("c_ropeS")
    c_out = chan("c_out")
    c_wkout = [chan(f"c_wko{i}") for i in range(NWK)]
    c_cache = chan("c_cache")

    wk_i = [0]

    def next_wk():
        i = wk_i[0] % NWK
        wk_i[0] += 1
        return i

    pt_i = [0]

    def next_pt():
        i = pt_i[0] % NPT
        pt_i[0] += 1
        return i

    bank_lo = [0]

    def next_bank():
        b = bank_lo[0] % 4
        bank_lo[0] += 1
        return b

    def A(t, a, b):
        return t[:, a:b]

    def mm(out, lhsT, rhs, start, stop, reads, writes, signal):
        return cx.op(PE, lambda: nc.tensor.matmul(out, lhsT, rhs, start=start, stop=stop),
                     reads=reads, writes=writes, signal=signal)

    def act(out, in_, func, reads, writes, bias=None, scale=None):
        kw = {}
        if bias is not None:
            kw["bias"] = bias
        if scale is not None:
            kw["scale"] = scale
        return cx.op(ACT, lambda: nc.scalar.activation(out=out, in_=in_, func=func, **kw),
                     reads=reads, writes=writes)

    def tt(E, out, in0, in1, op, reads, writes):
        return cx.op(E, lambda: E.e.tensor_tensor(out=out, in0=in0, in1=in1, op=op),
                     reads=reads, writes=writes)

    def tcopy(E, out, in_, reads, writes):
        return cx.op(E, lambda: E.e.tensor_copy(out=out, in_=in_), reads=reads, writes=writes)

    def stt(out, in0, scalar, in1, op0, op1, reads, writes):
        return cx.op(DVE, lambda: nc.vector.scalar_tensor_tensor(out=out, in0=in0, scalar=scalar, in1=in1,
                                                                 op0=op0, op1=op1),
                     reads=reads, writes=writes)

    def ts(out, in0, s1_, op0, reads, writes, s2_=None, op1=None):
        if op1 is None:
            return cx.op(DVE, lambda: nc.vector.tensor_scalar(out=out, in0=in0, scalar1=s1_, scalar2=None, op0=op0),
                         reads=reads, writes=writes)
        return cx.op(DVE, lambda: nc.vector.tensor_scalar(out=out, in0=in0, scalar1=s1_, scalar2=s2_, op0=op0, op1=op1),
                     reads=reads, writes=writes)

    npass = NT + (1 if DO_SAMPLE else 0)
    main_g = list(range(NADA, NGRP))

    def ada_g(u):
        return list(range(u * 24, (u + 1) * 24))
    order = ada_g(0) + main_g[0:24] + ada_g(1) + main_g[24:32] + ada_g(2) + main_g[32:110] + ada_g(3) + main_g[110:182]
    for _ in range(npass - 1):
        order += main_g
    ws = {"load": 0, "use": 0}
    cast_seq = []
    _seen = set()
    for g in order:
        if g not in _seen:
            _seen.add(g)
            cast_seq.append(g)
    cast_pos = {g: i for i, g in enumerate(cast_seq)}

    def cast_key(g):
        return ("wscr", cast_pos[g] // SG)

    NFIRST = NADA + NMAIN

    def ws_load():
        i = ws["load"]
        if i >= len(order):
            return
        g = order[i]
        b = i % NBUF
        if i < NFIRST:
            cx.dma(POOL, c_wsw[b], out=wbuf[b][:, :].rearrange("p (a b) -> p a b", b=2048),
                   in_=wall[g].rearrange("p (a b) -> p a b", b=2048), writes=[("wbuf", b)])
            if g >= NADA and npass > 1:
                cx.dma(SP, c_scr[b], out=wscr_g(g), in_=wbuf[b][:, :], reads=[("wbuf", b)], writes=[("wscr", g)])
        else:
            cx.dma(SP, wchan[b], out=wbuf[b][:, :], in_=wscr_g(g), reads=[("wscr", g)], writes=[("wbuf", b)])
        ws["load"] += 1

    def ws_use():
        b = ws["use"] % NBUF
        ws["use"] += 1
        return b

    def cload(dst, src, key):
        cx.dma(SP, c_const, out=dst, in_=src, writes=[key])

    cload(ident[:, :], ident_d[:, :], "ident")
    cload(permT[:, :], perm_d[:, :], "permT")
    cload(lnT[:, :], lnT_d[:, :], "lnT")
    cload(badaT[:, :], bada_d[:, :], "badaT")
    cload(cT[:, :], cT_d[:, :], "cT")
    cload(tabc[:, :], tabc_d[:, :], "tabc")
    cload(sinkT[:, :], sink_d[:, :], "sinkT")
    for k_ in ["ident", "permT", "lnT", "badaT", "cT", "tabc", "sinkT"]:
        cx.st[k_] = [(c_const, c_const.cnt), []]

    STOPPED = [False]
    try:
        ckpt(0)
    except StopBuild:
        STOPPED[0] = True
    if not STOPPED[0]:
        for _ in range(NBUF):
            ws_load()

    cx.op(DVE, lambda: nc.vector.memset(onesD[:, :], 1.0 / D), writes=["onesD"])
    cx.op(DVE, lambda: nc.vector.memset(onesb[:, :], 1.0), writes=["onesb"])
    cx.op(DVE, lambda: nc.vector.memset(zb[:, :], 0.0), writes=["zb"])
    cx.op(DVE, lambda: nc.vector.memset(epsT[:, :], EPS), writes=["epsT"])

    act(sig[:, :], cT[:, :], AF.Sigmoid, reads=["cT"], writes=["sig"])
    tt(DVE, scT[:, :], sig[:, :], cT[:, :], ALU.mult, reads=["sig", "cT"], writes=["scT"])
    act(esink[:, :], sinkT[:, :], AF.Exp, reads=["sinkT"], writes=["esink"])

    def modv(u, j, part):
        o = (u * 2 + j) * 48 + part * 16
        return MOD[:, o:o + 16]

    slot_i = [0]

    def new_slot():
        i = slot_i[0]
        slot_i[0] += 1
        return SC[:, i * 16:(i + 1) * 16]

    def lnG(u):
        return lnT[:, u * 16:(u + 1) * 16]

    def lnB(u):
        return lnT[:, 64 + u * 16:64 + (u + 1) * 16]

    OPS = {}
    HA = {}
    HB = {}
    XA = {}
    XB = {}
    A0 = {}
    for u in range(4):
        for j in range(2):
            OPS[(u, j)] = new_slot()
            if u >= 1:
                HA[(u, j)] = new_slot()
                HB[(u, j)] = new_slot()
        XA[u] = new_slot()
        XB[u] = new_slot()
    for j in range(2):
        A0[j] = new_slot()
    for u in range(4):
        f = ALPHA if u < 3 else 1.0
        ts(XA[u], lnG(u), f, ALU.mult, reads=["lnT"], writes=[("XA", u)])
        ts(XB[u], lnB(u), f, ALU.mult, reads=["lnT"], writes=[("XB", u)])

    def do_ada(u):
        bank = next_bank()
        for g in range(24):
            b = ws_use()
            for jj in range(2):
                chunk = 2 * g + jj
                for kc in range(KC):
                    last = (kc == KC - 1)
                    mm(ps[bank][:, chunk * 2:chunk * 2 + 2],
                       wbuf[b][:, (jj * 16 + kc) * 128:(jj * 16 + kc + 1) * 128],
                       scT[:, kc * 2:kc * 2 + 2], kc == 0, last,
                       reads=[("wbuf", b), "scT"], writes=[("ps", bank)], signal=last)
            ws_load()
        for j in range(2):
            o = (u * 2 + j) * 48
            src = ps[bank][:, 0:96].rearrange("p (c j) -> p c j", j=2)[:, :, j]
            tt(DVE, MOD[:, o:o + 48], src, badaT[:, u * 48:(u + 1) * 48], ALU.add,
               reads=[("ps", bank), "badaT"], writes=[("MOD", u, j)])
            ts(OPS[(u, j)], modv(u, j, 1), 1.0, ALU.add, reads=[("MOD", u, j)], writes=[("OPS", u, j)])
            if u == 0:
                ts(A0[j], OPS[(0, j)], 1.0 / ALPHA, ALU.mult, reads=[("OPS", 0, j)], writes=[("A0", j)])
            else:
                tt(DVE, HA[(u, j)], lnG(u - 1), OPS[(u, j)], ALU.mult, reads=["lnT", ("OPS", u, j)], writes=[("HA", u, j)])
                tt(DVE, HB[(u, j)], lnB(u - 1), OPS[(u, j)], ALU.mult, reads=["lnT", ("OPS", u, j)], writes=[("HB", u, j)])
                tt(DVE, HB[(u, j)], HB[(u, j)], modv(u, j, 0), ALU.add, reads=[("HB", u, j), ("MOD", u, j)],
                   writes=[("HB", u, j)])

    tcopy(DVE, identb[:, :], ident[:, :], reads=["ident"], writes=["identb"])
    tcopy(DVE, permTb[:, :], permT[:, :], reads=["permT"], writes=["permTb"])
    for h in range(16):
        gi = h % 2
        cx.dma(SP, c_Gf, out=Gf[:, :], in_=G_d[h], writes=["Gf"])
        act(Ghl[gi][:, 0:320], Gf[:, :], AF.Identity, reads=["Gf"], writes=[("G", gi)])
        w = next_wk()
        tt(DVE, wk[w][:, 0:320], Gf[:, :], Ghl[gi][:, 0:320], ALU.subtract, reads=["Gf", ("G", gi)], writes=[("wk", w)])
        act(Ghl[gi][:, 320:640], wk[w][:, 0:320], AF.Identity, reads=[("wk", w)], writes=[("G", gi)])
        cx.dma(SP, c_Gs[gi], out=Gscr[h], in_=Ghl[gi][:, :], reads=[("G", gi)], writes=[("Gscr", h)])

    do_ada(0)

    def xa_c(k, T):
        return xa[:, k * TT:k * TT + T]

    def r1_c(k, T):
        return R1[:, k * TT:k * TT + T]

    def q_c(k, T):
        return R2[:, k * TT:k * TT + T]

    def stg(j):
        return R2f[:, j * 2048:(j + 1) * 2048]

    def stg_keys(j):
        return [("q", k) for k in range(8 * j, 8 * j + 8)]

    def hid_c(hc, T):
        if hc < 16:
            return KA[:, 8192 + hc * TT:8192 + hc * TT + T]
        return VA[:, 8192 + (hc - 16) * TT:8192 + (hc - 16) * TT + T]

    def hid_key(hc):
        if hc < 16:
            return ("KA", 1, hc)
        return ("VA", 1, (hc - 16) // 4)

    def proj_stat(ngroups, rhs_fn, rhs_keys, T, evac, kcs=16, chunks_per_group=2):
        for g in range(ngroups):
            b = ws_use()
            for jj in range(chunks_per_group):
                chunk = chunks_per_group * g + jj
                bank = next_bank()
                for kc in range(kcs):
                    last = (kc == kcs - 1)
                    mm(ps[bank][:, 0:T], wbuf[b][:, (jj * kcs + kc) * 128:(jj * kcs + kc + 1) * 128], rhs_fn(kc),
                       kc == 0, last, reads=[("wbuf", b), rhs_keys(kc)], writes=[("ps", bank)], signal=last)
                evac(chunk, bank)
            ws_load()

    def proj_mov(ngroups, lhs_fn, lhs_keys, ntb, tbw, evac):
        for g in range(ngroups):
            b = ws_use()
            for tb in range(ntb):
                bank = next_bank()
                for kc in range(KC):
                    last = (kc == KC - 1)
                    mm(ps[bank][0:tbw, 0:256], lhs_fn(kc, tb), wbuf[b][:, kc * 256:(kc + 1) * 256],
                       kc == 0, last, reads=[("wbuf", b), lhs_keys(kc)], writes=[("ps", bank)], signal=last)
                evac(g, tb, bank)
            ws_load()

    def out_small(dst_ap, src_ap, wki):
        cx.dma(SP, c_wkout[wki], out=dst_ap, in_=src_ap, reads=[("wk", wki)], writes=[])

    def resid_and_stats(n, bank, T, gate_ap, j, u, do_stats):
        stt(xa_c(n, T), ps[bank][:, 0:T], gate_ap[:, n:n + 1], xa_c(n, T), ALU.mult, ALU.add,
            reads=[("ps", bank), ("xa", n), ("MOD", u, j)], writes=[("xa", n)])
        if do_stats:
            if n == 0:
                act(s2[:, 0:T], xa_c(n, T), AF.Square, reads=[("xa", n)], writes=["s2"])
                tcopy(DVE, s1[:, 0:T], xa_c(n, T), reads=[("xa", n)], writes=["s1"])
            else:
                w = next_wk()
                act(wk[w][:, 0:T], xa_c(n, T), AF.Square, reads=[("xa", n)], writes=[("wk", w)])
                tt(PL[0], s1[:, 0:T], s1[:, 0:T], xa_c(n, T), ALU.add, reads=["s1", ("xa", n)], writes=["s1"])
                tt(PL[0], s2[:, 0:T], s2[:, 0:T], wk[w][:, 0:T], ALU.add, reads=["s2", ("wk", w)], writes=["s2"])

    def ln_finish(T, u, j, final):
        b1 = next_bank()
        mm(ps[b1][:, 0:T], onesD[:, :], s1[:, 0:T], True, True, reads=["onesD", "s1"], writes=[("ps", b1)], signal=True)
        b2 = next_bank()
        mm(ps[b2][:, 0:T], onesD[:, :], s2[:, 0:T], True, True, reads=["onesD", "s2"], writes=[("ps", b2)], signal=True)
        tcopy(DVE, msb[:, 0:T], ps[b1][:, 0:T], reads=[("ps", b1)], writes=["msb"])
        tt(DVE, s1[:, 0:T], msb[:, 0:T], msb[:, 0:T], ALU.mult, reads=["msb"], writes=["s1"])
        tt(DVE, s1[:, 0:T], ps[b2][:, 0:T], s1[:, 0:T], ALU.subtract, reads=[("ps", b2), "s1"], writes=["s1"])
        act(s2[:, 0:T], s1[:, 0:T], AF.Ln, reads=["s1", "epsT"], writes=["s2"], bias=epsT[:, 0:1])
        act(rstd[:, 0:T], s2[:, 0:T], AF.Exp, reads=["s2"], writes=["rstd"], scale=-0.5)
        stt(nmr[:, 0:T], msb[:, 0:T], -1.0, rstd[:, 0:T], ALU.mult, ALU.mult, reads=["msb", "rstd"], writes=["nmr"])
        for k in range(KC):
            w = next_wk()
            tt(DVE, wk[w][:, 0:T], xa_c(k, T), rstd[:, 0:T], ALU.mult, reads=[("xa", k), "rstd"], writes=[("wk", w)])
            tt(PL[0], wk[w][:, 0:T], wk[w][:, 0:T], nmr[:, 0:T], ALU.add, reads=[("wk", w), "nmr"], writes=[("wk", w)])
            act(xa_c(k, T), wk[w][:, 0:T], AF.Identity, reads=[("wk", w), ("XA", u), ("XB", u)], writes=[("xa", k)],
                scale=XA[u][:, k:k + 1], bias=XB[u][:, k:k + 1])
            if not final:
                act(r1_c(k, T), wk[w][:, 0:T], AF.Identity, reads=[("wk", w), ("HA", u + 1, j), ("HB", u + 1, j)],
                    writes=[("R1", k)], scale=HA[(u + 1, j)][:, k:k + 1], bias=HB[(u + 1, j)][:, k:k + 1])

    def mlp(T, u, j, final):
        gate_ap = modv(u, j, 2)
        for half in range(2):
            def evac_up(chunk, bank):
                w = next_wk()
                act(wk[w][:, 0:T], ps[bank][:, 0:T], AF.Relu, reads=[("ps", bank)], writes=[("wk", w)])
                tt(DVE, hid_c(chunk, T), wk[w][:, 0:T], wk[w][:, 0:T], ALU.mult, reads=[("wk", w)],
                   writes=[hid_key(chunk)])
            proj_stat(16, lambda kc: r1_c(kc, T), lambda kc: ("R1", kc), T, evac_up)

            def evac_dn(n, bank):
                resid_and_stats(n, bank, T, gate_ap, j, u, do_stats=(half == 1))
            proj_stat(16, lambda kc: hid_c(kc, T), lambda kc: hid_key(kc), T, evac_dn, kcs=32, chunks_per_group=1)
        ln_finish(T, u, j, final)

    def oproj(T, u, j):
        gate_ap = modv(u, j, 2)

        def evac(n, bank):
            resid_and_stats(n, bank, T, gate_ap, j, u, do_stats=True)
        proj_stat(8, lambda kc: r1_c(kc, T), lambda kc: ("R1", kc), T, evac)
        ln_finish(T, u, j, False)

    def out_transposed(src_fn, src_keys, nfc, tbw, dst_fn, width_cols=128):
        for fc0 in range(0, nfc, 4):
            nf = min(4, nfc - fc0)
            bank = next_bank()
            for i in range(nf):
                cx.op(PE, lambda i=i: nc.tensor.transpose(ps[bank][0:tbw, i * 128:(i + 1) * 128], src_fn(fc0 + i), ident[:, :]),
                      reads=[src_keys(fc0 + i), "ident"], writes=[("ps", bank)], signal=(i == nf - 1))
            w = next_wk()
            act(wk[w][0:tbw, 0:nf * 128], ps[bank][0:tbw, 0:nf * 128], AF.Identity, reads=[("ps", bank)], writes=[("wk", w)])
            dst_fn(fc0, nf, w)

    PL = [DVE]
    PREF = set()
    def run_pass(pi):
        PASS_MARK.append((pi, getattr(PE, 'nops', 0)))
        is_sample = (pi == 8)
        PL[0] = DVE if pi == 0 else POOL
        j = 1 if is_sample else 0
        T = 16 if is_sample else TT
        ntb = 1 if is_sample else 4
        tbw = 16 if is_sample else 128
        nqc = 1 if is_sample else 8
        qcw = 16 if is_sample else 64
        has_hist = is_sample or pi > 0
        is_last = is_sample or pi == 7
        xsrc = xs if is_sample else xp
        row0 = 0 if is_sample else pi * TT

        pre = pi in PREF
        if not pre:
            cx.dma(SP, c_ropeC, out=ropeC[:, :], in_=ropeC_d[pi], writes=["ropeC"])
            cx.dma(SP, c_ropeS, out=ropeS[:, :], in_=ropeS_d[pi], writes=["ropeS"])

        if is_sample:
            for blk in range(4):
                cx.dma(POOL, c_cache, out=VA[:, blk * 2048:(blk + 1) * 2048], in_=cav[blk * 128:(blk + 1) * 128, :],
                       writes=[("VA", 0, blk)])
            cx.dma(POOL, c_cache, out=VB[:, 0:512], in_=cbv[:, :], writes=[("VB", 0)])
            for k_ in [("VA", 0, 0), ("VA", 0, 1), ("VA", 0, 2), ("VA", 0, 3), ("VB", 0)]:
                cx.st[k_] = [(c_cache, c_cache.cnt), []]
            for blk in range(4):
                s = blk % 2
                cx.dma(SP, c_stg[s], out=stg(s), in_=cak[blk * 128:(blk + 1) * 128, :], writes=stg_keys(s))
                for h0 in range(0, 16, 4):
                    bank = next_bank()
                    for i in range(4):
                        h = h0 + i
                        cx.op(PE, lambda i=i, h=h, s=s: nc.tensor.transpose(ps[bank][:, i * 128:(i + 1) * 128],
                                                                             stg(s)[:, h * 128:(h + 1) * 128], ident[:, :]),
                              reads=stg_keys(s) + ["ident"], writes=[("ps", bank)], signal=(i == 3))
                    dst = KA[:, 0:8192].rearrange("p (h t) -> p h t", t=TT)[:, h0:h0 + 4, blk * 128:(blk + 1) * 128]
                    src = ps[bank][:, :].rearrange("p (h t) -> p h t", t=128)
                    cx.op(ACT, lambda dst=dst, src=src: nc.scalar.activation(out=dst, in_=src, func=AF.Identity),
                          reads=[("ps", bank)], writes=[("KA", 0, h0 + i) for i in range(4)])
            s = 0
            stgv = stg(s)[:, 0:1024].rearrange("p (g r d) -> p g r d", r=2, d=64)
            srcv = cbk[:, :].rearrange("p (g d) -> p g d", d=64)
            for r in range(2):
                cx.dma(SP, c_stg[s], out=stgv[:, :, r, :], in_=srcv, writes=stg_keys(s))
            for g0 in range(0, 8, 4):
                bank = next_bank()
                for i in range(4):
                    g = g0 + i
                    cx.op(PE, lambda i=i, g=g: nc.tensor.transpose(ps[bank][:, i * 128:(i + 1) * 128],
                                                                    stg(0)[:, g * 128:(g + 1) * 128], ident[:, :]),
                          reads=stg_keys(0) + ["ident"], writes=[("ps", bank)], signal=(i == 3))
                dst = KB[:, :].rearrange("p (g t) -> p g t", t=640)[:, g0:g0 + 4, 0:128]
                src = ps[bank][:, :].rearrange("p (g t) -> p g t", t=128)
                cx.op(ACT, lambda dst=dst, src=src: nc.scalar.activation(out=dst, in_=src, func=AF.Identity),
                      reads=[("ps", bank)], writes=[("KB", g0 + i) for i in range(4)])

        for tb in range(ntb):
            s = tb % 2
            if not (pre and tb < 2):
                cx.dma(SP, c_stg[s], out=stg(s)[0:tbw, :], in_=xsrc[row0 + tb * tbw:row0 + (tb + 1) * tbw, :],
                       writes=stg_keys(s))
            for k0 in range(0, KC, 4):
                bank = next_bank()
                for i in range(4):
                    k = k0 + i
                    cx.op(PE, lambda i=i, k=k, s=s: nc.tensor.transpose(ps[bank][:, i * 128:i * 128 + tbw],
                                                                         stg(s)[0:tbw, k * 128:(k + 1) * 128],
                                                                         ident[0:tbw, 0:tbw]),
                          reads=stg_keys(s) + ["ident"], writes=[("ps", bank)], signal=(i == 3))
                for i in range(4):
                    k = k0 + i
                    src = ps[bank][:, i * 128:i * 128 + tbw]
                    dsl = slice(k * TT + tb * tbw, k * TT + (tb + 1) * tbw)
                    cx.op(DVE, lambda dsl=dsl, src=src: nc.vector.tensor_scalar(
                        out=xa[:, dsl], in0=src, scalar1=ALPHA, scalar2=None, op0=ALU.mult),
                        reads=[("ps", bank)], writes=[("xa", k)])
                    cx.op(ACT, lambda k=k, dsl=dsl: nc.scalar.activation(
                        out=R1[:, dsl], in_=xa[:, dsl], func=AF.Identity,
                        scale=A0[j][:, k:k + 1], bias=modv(0, j, 0)[:, k:k + 1]),
                        reads=[("xa", k), ("A0", j), ("MOD", 0, j)], writes=[("R1", k)])

        ckpt(4)
        scaleA = float(128 ** -0.5)

        def evac_qa(h, bank):
            act(q_c(h, T), ps[bank][:, 0:T], AF.Identity, reads=[("ps", bank)], writes=[("q", h)], scale=scaleA)
        proj_stat(8, lambda kc: r1_c(kc, T), lambda kc: ("R1", kc), T, evac_qa)

        def evac_ka(h, bank):
            tcopy(DVE, KA[:, 8192 + h * TT:8192 + h * TT + T], ps[bank][:, 0:T], reads=[("ps", bank)],
                  writes=[("KA", 1, h)])
            if is_last:
                w = next_wk()
                tcopy(DVE, wk[w][:, 0:T], ps[bank][:, 0:T], reads=[("ps", bank)], writes=[("wk", w)])
                dsto = ak_s if is_sample else ak_p
                bk2 = next_bank()
                for tb in range(ntb):
                    cx.op(PE, lambda tb=tb, w=w: nc.tensor.transpose(ps[bk2][0:tbw, tb * 128:(tb + 1) * 128],
                                                                      wk[w][:, tb * tbw:(tb + 1) * tbw], ident[:, :]),
                          reads=[("wk", w), "ident"], writes=[("ps", bk2)], signal=(tb == ntb - 1))
                w2 = next_wk()
                act(wk[w2][0:tbw, 0:ntb * 128], ps[bk2][0:tbw, 0:ntb * 128], AF.Identity, reads=[("ps", bk2)],
                    writes=[("wk", w2)])
                dst = dsto[:, h * 128:(h + 1) * 128].rearrange("(tb p) d -> p tb d", p=tbw)
                src = wk[w2][0:tbw, 0:ntb * 128].rearrange("p (tb d) -> p tb d", d=128)
                out_small(dst, src, w2)
        proj_stat(8, lambda kc: r1_c(kc, T), lambda kc: ("R1", kc), T, evac_ka)

        def evac_va(g, tb, bank):
            tcopy(DVE, VA[0:tbw, 8192 + tb * 2048 + g * 256:8192 + tb * 2048 + (g + 1) * 256], ps[bank][0:tbw, 0:256],
                  reads=[("ps", bank)], writes=[("VA", 1, tb)])
            if is_last:
                w = next_wk()
                tcopy(DVE, wk[w][0:tbw, 0:256], ps[bank][0:tbw, 0:256], reads=[("ps", bank)], writes=[("wk", w)])
                dsto = av_s if is_sample else av_p
                out_small(dsto[tb * tbw:(tb + 1) * tbw, g * 256:(g + 1) * 256], wk[w][0:tbw, 0:256], w)
        proj_mov(8, lambda kc, tb: R1[:, kc * TT + tb * tbw:kc * TT + (tb + 1) * tbw], lambda kc: ("R1", kc), ntb, tbw,
                 evac_va)

        if pi == 0:
            do_ada(1)
        ckpt(5)
        kbs = []
        if has_hist:
            for i in range(4):
                kbs.append((0, i, 128, 2 * i, True))
        for i in range(ntb):
            kbs.append((1, i, tbw, 8 + 2 * i, not is_sample))
        pend = []

        def flush_pend():
            while pend:
                pend.pop(0)()

        for h in range(16):
            gi = h % 2
            cx.dma(SP, c_G[gi], out=Ghl[gi][:, :], in_=Gscr[h], reads=[("Gscr", h)], writes=[("G", gi)])
            pvb = 4 + 2 * (h % 2)
            dnb = pvb + 1
            mm(ps[pvb][:, 0:T], zb[:, 0:128], zb[:, 0:T], True, False, reads=["zb"], writes=[("ps", pvb)], signal=False)
            mm(ps[dnb][:, 0:T], zb[:, 0:128], zb[:, 0:T], True, False, reads=["zb"], writes=[("ps", dnb)], signal=False)
            nk_list = []
            for (slot, blk, nk, e0, two) in kbs:
                c_lo = max(0, e0 - 8)
                c_hi = min(nqc - 1, e0 + 1 if two else e0)
                if c_hi < c_lo:
                    continue
                nk_list.append((slot, blk, nk, e0, two, c_lo, c_hi))
            for idx, (slot, blk, nk, e0, two, c_lo, c_hi) in enumerate(nk_list):
                lastkb = (idx == len(nk_list) - 1)
                ncols = (c_hi - c_lo + 1) * qcw
                q0 = c_lo * qcw
                sb_ = next_bank()
                m0 = 64 * (9 - e0) + q0
                nn = max(0, min(ncols, 320 - m0))
                mm(ps[sb_][0:nk, 0:ncols], KA[:, slot * 8192 + h * TT + blk * 128:slot * 8192 + h * TT + blk * 128 + nk],
                   R2[:, h * TT + q0:h * TT + q0 + ncols], True, nn == 0,
                   reads=[("KA", slot, h), ("q", h)], writes=[("ps", sb_)], signal=(nn == 0))
                if nn > 0:
                    mm(ps[sb_][0:nk, 0:nn], identb[0:nk, 0:nk], Ghl[gi][0:nk, m0:m0 + nn], False, False,
                       reads=["identb", ("G", gi)], writes=[("ps", sb_)], signal=False)
                    mm(ps[sb_][0:nk, 0:nn], identb[0:nk, 0:nk], Ghl[gi][0:nk, 320 + m0:320 + m0 + nn], False, True,
                       reads=["identb", ("G", gi)], writes=[("ps", sb_)], signal=True)
                p = next_pt()
                if nn > 0:
                    act(PT[p][0:nk, 0:nn], ps[sb_][0:nk, 0:nn], AF.Exp, reads=[("ps", sb_)], writes=[("PT", p)])
                if nn < ncols:
                    act(PT[p][0:nk, nn:ncols], ps[sb_][0:nk, nn:ncols], AF.Exp, reads=[("ps", sb_), "tabc"],
                        writes=[("PT", p)], bias=tabc[0:nk, h:h + 1])
                if two:
                    c = e0 + 1
                    if c_lo <= c <= c_hi:
                        cx.op(PL[0], lambda p=p, c=c: PL[0].e.memset(PT[p][0:64, (c - c_lo) * qcw:(c - c_lo + 1) * qcw], 0.0),
                              writes=[("PT", p)])
                    c = e0 - 8
                    if c_lo <= c <= c_hi:
                        cx.op(PL[0], lambda p=p, c=c: PL[0].e.memset(PT[p][64:128, (c - c_lo) * qcw:(c - c_lo + 1) * qcw], 0.0),
                              writes=[("PT", p)])

                def pv(slot=slot, blk=blk, nk=nk, p=p, q0=q0, ncols=ncols, lastkb=lastkb, pvb=pvb, dnb=dnb, h=h):
                    mm(ps[pvb][:, q0:q0 + ncols], VA[0:nk, slot * 8192 + blk * 2048 + h * 128:slot * 8192 + blk * 2048 + (h + 1) * 128],
                       PT[p][0:nk, 0:ncols], False, lastkb, reads=[("VA", slot, blk), ("PT", p)], writes=[("ps", pvb)],
                       signal=lastkb)
                    mm(ps[dnb][:, q0:q0 + ncols], onesb[0:nk, :], PT[p][0:nk, 0:ncols], False, lastkb,
                       reads=["onesb", ("PT", p)], writes=[("ps", dnb)], signal=lastkb)
                    if lastkb:
                        wl = next_wk()
                        act(wk[wl][:, 0:T], ps[dnb][:, 0:T], AF.Ln, reads=[("ps", dnb)], writes=[("wk", wl)])
                        act(rden[:, 0:T], wk[wl][:, 0:T], AF.Exp, reads=[("wk", wl)], writes=["nmr"], scale=-1.0)
                        tt(DVE, r1_c(h, T), ps[pvb][:, 0:T], rden[:, 0:T], ALU.mult, reads=[("ps", pvb), "nmr"],
                           writes=[("R1", h)])
                pend.append(pv)
                if len(pend) > min(LOOK, len(kbs) - 1):
                    pend.pop(0)()
        flush_pend()

        if not is_sample and pi < NT - 1:
            for h in range(16):
                pass
            cx.op(DVE, lambda: nc.vector.tensor_copy(out=KA[:, 0:8192], in_=KA[:, 8192:16384]),
                  reads=[("KA", 1, h) for h in range(16)], writes=[("KA", 0, h) for h in range(16)])
            cx.op(DVE, lambda: nc.vector.tensor_copy(out=VA[:, 0:8192], in_=VA[:, 8192:16384]),
                  reads=[("VA", 1, b) for b in range(4)], writes=[("VA", 0, b) for b in range(4)])

        ckpt(6)
        oproj(T, 0, j)
        if pi == 0:
            do_ada(2)
        ckpt(7)
        mlp(T, 1, j, False)
        ckpt(8)

        def rope_evac(dst_ap, dst_key, bank, qscale, keep_f32=None):
            w0 = next_wk()
            act(wk[w0][:, 0:T], ps[bank][:, 0:T], AF.Identity, reads=[("ps", bank)], writes=[("wk", w0)], scale=qscale)
            ph = next_pt()
            act(PT[ph][:, 0:T], ps[bank][:, 0:T], AF.Identity, reads=[("ps", bank)], writes=[("PT", ph)], scale=qscale)
            b2 = next_bank()
            mm(ps[b2][:, 0:T], permTb[:, :], PT[ph][:, 0:T], True, True, reads=["permTb", ("PT", ph)], writes=[("ps", b2)],
               signal=True)
            w1 = next_wk()
            tt(DVE, wk[w1][:, 0:T], ps[b2][:, 0:T], ropeS[:, 0:T], ALU.mult, reads=[("ps", b2), "ropeS"], writes=[("wk", w1)])
            tt(PL[0], wk[w0][:, 0:T], wk[w0][:, 0:T], ropeC[:, 0:T], ALU.mult, reads=[("wk", w0), "ropeC"], writes=[("wk", w0)])
            if keep_f32 is None:
                tt(DVE, dst_ap, wk[w0][:, 0:T], wk[w1][:, 0:T], ALU.add, reads=[("wk", w0), ("wk", w1)], writes=[dst_key])
            else:
                tt(DVE, wk[w0][:, 0:T], wk[w0][:, 0:T], wk[w1][:, 0:T], ALU.add, reads=[("wk", w0), ("wk", w1)],
                   writes=[("wk", w0)])
                tcopy(PL[0], dst_ap, wk[w0][:, 0:T], reads=[("wk", w0)], writes=[dst_key])
                keep_f32(w0)

        def evac_qb(i, bank):
            rope_evac(q_c(i, T), ("q", i), bank, 0.125)
        proj_stat(8, lambda kc: r1_c(kc, T), lambda kc: ("R1", kc), T, evac_qb)

        def evac_kb(g, bank):
            dst = KB[:, g * 640 + 128:g * 640 + 128 + T]
            if is_last:
                def keep(w0):
                    t0 = T - tbw
                    bk2 = next_bank()
                    cx.op(PE, lambda: nc.tensor.transpose(ps[bk2][0:tbw, 0:128], wk[w0][:, t0:t0 + tbw], ident[:, :]),
                          reads=[("wk", w0), "ident"], writes=[("ps", bk2)], signal=True)
                    w2 = next_wk()
                    act(wk[w2][0:tbw, 0:64], ps[bk2][0:tbw, 0:64], AF.Identity, reads=[("ps", bk2)], writes=[("wk", w2)])
                    dsto = bk_s if is_sample else bk_p
                    out_small(dsto[:, g * 64:(g + 1) * 64], wk[w2][0:tbw, 0:64], w2)
                rope_evac(dst, ("KB", g), bank, 1.0, keep_f32=keep)
            else:
                rope_evac(dst, ("KB", g), bank, 1.0)
        proj_stat(4, lambda kc: r1_c(kc, T), lambda kc: ("R1", kc), T, evac_kb)

        def evac_vb(g, tb, bank):
            tcopy(DVE, VB[0:tbw, (1 + tb) * 512 + g * 256:(1 + tb) * 512 + (g + 1) * 256], ps[bank][0:tbw, 0:256],
                  reads=[("ps", bank)], writes=[("VB", 1 + tb)])
            if is_last and tb == ntb - 1:
                w = next_wk()
                tcopy(DVE, wk[w][0:tbw, 0:256], ps[bank][0:tbw, 0:256], reads=[("ps", bank)], writes=[("wk", w)])
                dsto = bv_s if is_sample else bv_p
                out_small(dsto[:, g * 256:(g + 1) * 256], wk[w][0:tbw, 0:256], w)
        proj_mov(2, lambda kc, tb: R1[:, kc * TT + tb * tbw:kc * TT + (tb + 1) * tbw], lambda kc: ("R1", kc), ntb, tbw,
                 evac_vb)

        if pi == 0:
            do_ada(3)
        ckpt(9)
        kbsB = []
        if has_hist:
            kbsB.append((0, 0, 128, 0, True))
        for i in range(ntb):
            kbsB.append((128 + i * 128, 1 + i, tbw, 2 + 2 * i, not is_sample))
        for pr in range(16):
            pvb = 4 + 2 * (pr % 2)
            dnb = pvb + 1
            g = pr // 2
            mm(ps[pvb][:, 0:T], zb[:, 0:128], zb[:, 0:T], True, False, reads=["zb"], writes=[("ps", pvb)], signal=False)
            mm(ps[dnb][:, 0:T], zb[:, 0:128], zb[:, 0:T], True, False, reads=["zb"], writes=[("ps", dnb)], signal=False)
            items = []
            for hf in range(2):
                for (col0, vblk, nk, e0, two) in kbsB:
                    c_lo = max(0, e0 - 2)
                    c_hi = min(nqc - 1, e0 + 1 if two else e0)
                    if c_hi < c_lo:
                        continue
                    items.append((hf, col0, vblk, nk, e0, two, c_lo, c_hi))
            for idx, (hf, col0, vblk, nk, e0, two, c_lo, c_hi) in enumerate(items):
                lastit = (idx == len(items) - 1)
                lasthalf = lastit or (items[idx + 1][0] != hf)
                ncols = (c_hi - c_lo + 1) * qcw
                q0 = c_lo * qcw
                p0 = 64 * hf
                sb_ = next_bank()
                mm(ps[sb_][0:nk, 0:ncols], KB[p0:p0 + 64, g * 640 + col0:g * 640 + col0 + nk],
                   R2[p0:p0 + 64, pr * TT + q0:pr * TT + q0 + ncols], True, True,
                   reads=[("KB", g), ("q", pr)], writes=[("ps", sb_)], signal=True)
                p = next_pt()
                act(PT[p][0:nk, 0:ncols], ps[sb_][0:nk, 0:ncols], AF.Exp, reads=[("ps", sb_)], writes=[("PT", p)])
                if two:
                    c = e0 + 1
                    if c_lo <= c <= c_hi:
                        cx.op(PL[0], lambda p=p, c=c, c_lo=c_lo: PL[0].e.memset(
                            PT[p][0:64, (c - c_lo) * qcw:(c - c_lo + 1) * qcw], 0.0), writes=[("PT", p)])
                    c = e0 - 2
                    if c_lo <= c <= c_hi:
                        cx.op(PL[0], lambda p=p, c=c, c_lo=c_lo: PL[0].e.memset(
                            PT[p][64:128, (c - c_lo) * qcw:(c - c_lo + 1) * qcw], 0.0), writes=[("PT", p)])

                def pvB(vblk=vblk, nk=nk, p=p, q0=q0, ncols=ncols, lastit=lastit, pvb=pvb, dnb=dnb, g=g, p0=p0, pr=pr,
                        lasthalf=lasthalf):
                    mm(ps[pvb][p0:p0 + 64, q0:q0 + ncols], VB[0:nk, vblk * 512 + g * 64:vblk * 512 + (g + 1) * 64],
                       PT[p][0:nk, 0:ncols], False, lasthalf, reads=[("VB", vblk), ("PT", p)], writes=[("ps", pvb)],
                       signal=lastit)
                    mm(ps[dnb][p0:p0 + 64, q0:q0 + ncols], onesb[0:nk, 0:64], PT[p][0:nk, 0:ncols], False, lasthalf,
                       reads=["onesb", ("PT", p)], writes=[("ps", dnb)], signal=lastit)
                    if lastit:
                        wl = next_wk()
                        act(wk[wl][:, 0:T], ps[dnb][:, 0:T], AF.Ln, reads=[("ps", dnb), "esink"], writes=[("wk", wl)],
                            bias=esink[:, pr:pr + 1])
                        act(rden[:, 0:T], wk[wl][:, 0:T], AF.Exp, reads=[("wk", wl)], writes=["nmr"], scale=-1.0)
                        tt(DVE, r1_c(pr, T), ps[pvb][:, 0:T], rden[:, 0:T], ALU.mult, reads=[("ps", pvb), "nmr"],
                           writes=[("R1", pr)])
                pend.append(pvB)
                if len(pend) > min(LOOK, 2 * len(kbsB) - 1):
                    pend.pop(0)()
        flush_pend()

        if not is_sample and pi < NT - 1:
            cx.dma(SP, c_ropeC, out=ropeC[:, :], in_=ropeC_d[pi + 1], writes=["ropeC"])
            cx.dma(SP, c_ropeS, out=ropeS[:, :], in_=ropeS_d[pi + 1], writes=["ropeS"])
            for tb_ in range(2):
                cx.dma(SP, c_stg[tb_], out=stg(tb_)[0:128, :],
                       in_=xp[(pi + 1) * TT + tb_ * 128:(pi + 1) * TT + (tb_ + 1) * 128, :], writes=stg_keys(tb_))
            PREF.add(pi + 1)
        if not is_sample and pi < NT - 1:
            kbv = KB[:, :].rearrange("p (g t) -> p g t", t=640)
            cx.op(PL[0], lambda: PL[0].e.tensor_copy(out=kbv[:, :, 0:128], in_=kbv[:, :, 512:640]),
                  reads=[("KB", g) for g in range(8)], writes=[("KB", g) for g in range(8)])
            cx.op(PL[0], lambda: PL[0].e.tensor_copy(out=VB[:, 0:512], in_=VB[:, 2048:2560]),
                  reads=[("VB", 4)], writes=[("VB", 0)])

        ckpt(10)
        oproj(T, 2, j)
        mlp(T, 3, j, True)
        ckpt(11)

        ydst = y_s if is_sample else y_p
        for tb in range(ntb):
            def dstf(fc0, nf, w, tb=tb):
                out_small(ydst[row0 + tb * tbw:row0 + (tb + 1) * tbw, fc0 * 128:(fc0 + nf) * 128], wk[w][0:tbw, 0:nf * 128], w)
            out_transposed(lambda fc, tb=tb: xa[:, fc * TT + tb * tbw:fc * TT + (tb + 1) * tbw], lambda fc: ("xa", fc),
                           16, tbw, dstf)

    try:
        ckpt(3)
        for pi in range(NT):
            run_pass(pi)
        if DO_SAMPLE:
            run_pass(8)
    except StopBuild:
        pass

    for ch in c_wkout:
        if ch.cnt:
            nc.sync.wait_ge(ch.sem, ch.cnt)
    return nc


_CACHE = {}


def kernel(x_prompt, x_sample, cache_a_k, cache_a_v, cache_b_k, cache_b_v, c_prompt, c_sample,
           w_ada, b_ada, ln_g, ln_b, w_qkv_a, w_o_a, rel_bias_a, w_qkv_b, w_o_b, sink_b, w_up, w_down,
           _NT=8, _DO_SAMPLE=True, _cores=8):
    f = lambda a: np.ascontiguousarray(np.asarray(a, dtype=np.float32))
    x_prompt, x_sample = f(x_prompt), f(x_sample)
    wall = build_weights(f(w_ada), f(w_qkv_a), f(w_o_a), f(w_qkv_b), f(w_o_b), f(w_up), f(w_down))
    ropeC, ropeS, permT = rope_tables()
    ident = np.eye(128, dtype=np.float32)
    lg = f(ln_g).reshape(4, 16, 128).transpose(2, 0, 1)
    lb = f(ln_b).reshape(4, 16, 128).transpose(2, 0, 1)
    lnT = np.ascontiguousarray(np.concatenate([lg.reshape(128, 64), lb.reshape(128, 64)], axis=1))
    badaT = np.ascontiguousarray(f(b_ada).reshape(4, 48, 128).transpose(2, 0, 1).reshape(128, 192))
    tab = f(rel_bias_a)[0]
    pp = np.arange(128)[:, None]
    mmi = np.arange(320)[None, :]
    idx = np.clip(mmi - pp - 64, -128, 128) + 128
    G = np.ascontiguousarray(tab[:, idx])
    tabc = np.ascontiguousarray(np.broadcast_to(tab[:, 256][None, :], (128, 16)))
    sk = f(sink_b)[0]
    sinkT = np.ascontiguousarray(np.stack([np.broadcast_to(sk[0::2][None], (64, 16)),
                                           np.broadcast_to(sk[1::2][None], (64, 16))], 0).reshape(128, 16))
    key = (_NT, _DO_SAMPLE)
    if key not in _CACHE:
        _CACHE[key] = build(_NT, _DO_SAMPLE)
    nc = _CACHE[key]
    in_maps = []
    for b in range(_cores):
        cc = np.stack([f(c_prompt)[b], f(c_sample)[b]], axis=-1)
        cT = np.ascontiguousarray(cc.reshape(16, 128, 2).transpose(1, 0, 2).reshape(128, 32))
        in_maps.append({
            "xp": x_prompt[b], "xs": x_sample[b], "cT": cT,
            "cak": f(cache_a_k)[0, b].reshape(512, 2048), "cav": f(cache_a_v)[0, b].reshape(512, 2048),
            "cbk": f(cache_b_k)[0, b].reshape(128, 512), "cbv": f(cache_b_v)[0, b].reshape(128, 512),
            "wall": wall, "lnT": lnT, "badaT": badaT, "G": G, "tabc": tabc, "sinkT": sinkT,
            "ident": ident, "permT": permT, "ropeC": ropeC, "ropeS": ropeS,
        })
    res = run_bass_kernel_spmd(nc, in_maps, core_ids=list(range(_cores)))
    R = res.results
    st = lambda n: np.stack([np.asarray(r[n], dtype=np.float32) for r in R], 0)
    y_p = st("y_p")
    y_s = st("y_s")
    ak_p = st("ak_p").reshape(_cores, 512, 16, 128)[None]
    av_p = st("av_p").reshape(_cores, 512, 16, 128)[None]
    bk_p = st("bk_p").reshape(_cores, 128, 8, 64)[None]
    bv_p = st("bv_p").reshape(_cores, 128, 8, 64)[None]
    ak_s = st("ak_s").reshape(_cores, 16, 16, 128)[None]
    av_s = st("av_s").reshape(_cores, 16, 16, 128)[None]
    bk_s = st("bk_s").reshape(_cores, 16, 8, 64)[None]
    bv_s = st("bv_s").reshape(_cores, 16, 8, 64)[None]
    return (y_p, y_s, ak_p, av_p, bk_p, bv_p, ak_s, av_s, bk_s, bv_s)
```

```python
import numpy as np
import concourse.bass as bass
import concourse.mybir as mybir
from concourse.bass_utils import run_bass_kernel_spmd

F32 = mybir.dt.float32
BF16 = mybir.dt.bfloat16
AF = mybir.ActivationFunctionType
ALU = mybir.AluOpType

D = 2048
KC = 16
TT = 512
ALPHA = float(4.0 ** 0.25)
EPS = 1e-5
NBUF = 3
NADA = 96
NMAIN = 182
NGRP = NADA + NMAIN
SG = 14
ROT = 16
THETA = 500000.0


class Eng:
    def __init__(self, e, name, sem, is_pe=False):
        self.e = e
        self.name = name
        self.sem = sem
        self.cnt = 0
        self.seen = {}
        self.is_pe = is_pe


class Chan:
    def __init__(self, sem, name):
        self.sem = sem
        self.cnt = 0
        self.name = name


PASS_MARK = []


class StopBuild(Exception):
    pass


import os as _os
_STOP = int(_os.environ.get("K_STOP", "99"))
_SUB = int(_os.environ.get("K_SUB", "99"))


def ckpt(n):
    if _STOP <= n:
        raise StopBuild()


class Ctx:
    def __init__(self, nc):
        self.nc = nc
        self.st = {}
        self.sems = []

    def sem(self, name):
        s = self.nc.alloc_semaphore(name)
        self.sems.append(s)
        return s

    def _wait(self, E, reads, writes):
        deps = {}
        for k in reads:
            st = self.st.get(k)
            if st and st[0] is not None:
                s, v = st[0]
                if deps.get(s, 0) < v:
                    deps[s] = v
        for k in writes:
            st = self.st.get(k)
            if st:
                if st[0] is not None:
                    s, v = st[0]
                    if deps.get(s, 0) < v:
                        deps[s] = v
                for (s, v) in st[1]:
                    if deps.get(s, 0) < v:
                        deps[s] = v
        for s, v in deps.items():
            if s is E and E.is_pe:
                continue
            if E.seen.get(s, 0) < v:
                E.e.wait_ge(s.sem, v)
                E.seen[s] = v

    def _upd(self, reads, writes, stamp):
        for k in reads:
            st = self.st.get(k)
            if st is None:
                self.st[k] = [None, [stamp]]
            else:
                st[1] = [x for x in st[1] if x[0] is not stamp[0]] + [stamp]
        for k in writes:
            self.st[k] = [stamp, []]

    def op(self, E, fn, reads=(), writes=(), signal=True):
        self._wait(E, reads, writes)
        ins = fn()
        E.nops = getattr(E, 'nops', 0) + 1
        if signal:
            ins.then_inc(E.sem, 1)
            E.cnt += 1
            stamp = (E, E.cnt)
        else:
            stamp = (E, E.cnt + 1)
        self._upd(reads, writes, stamp)
        return ins

    def dma(self, Q, chan, out, in_, reads=(), writes=(), **kw):
        self._wait(Q, reads, writes)
        ins = Q.e.dma_start(out=out, in_=in_, **kw)
        ins.then_inc(chan.sem, 16)
        chan.cnt += 16
        self._upd(reads, writes, (chan, chan.cnt))
        return ins


def stat_groups(W):
    K, N = W.shape
    kc = K // 128
    assert kc == 16 and N % 256 == 0
    G = N // 256
    a = W.reshape(kc, 128, G, 2, 128)
    a = a.transpose(2, 1, 3, 0, 4)
    return np.ascontiguousarray(a).reshape(G, 128, 4096)


def mov_groups(W):
    K, N = W.shape
    kc = K // 128
    G = N // 256
    a = W.reshape(kc, 128, G, 256)
    a = a.transpose(2, 1, 0, 3)
    return np.ascontiguousarray(a).reshape(G, 128, 4096)


def down_groups(W, half):
    Wh = W[half * 4096:(half + 1) * 4096]
    a = Wh.reshape(32, 128, 16, 128)
    a = a.transpose(2, 1, 0, 3)
    return np.ascontiguousarray(a).reshape(16, 128, 4096)


def build_weights(w_ada, w_qkv_a, w_o_a, w_qkv_b, w_o_b, w_up, w_down):
    gs = []
    for u in range(4):
        gs.append(stat_groups(w_ada[u // 2, u % 2]))
    qa = w_qkv_a[0]
    gs.append(stat_groups(qa[:, 0:2048]))
    gs.append(stat_groups(qa[:, 2048:4096]))
    gs.append(mov_groups(qa[:, 4096:6144]))
    gs.append(stat_groups(w_o_a[0]))
    for h in range(2):
        gs.append(stat_groups(w_up[0][:, h * 4096:(h + 1) * 4096]))
        gs.append(down_groups(w_down[0], h))
    qb = w_qkv_b[0]
    gs.append(stat_groups(qb[:, 0:2048]))
    kb = qb[:, 2048:2560].reshape(2048, 8, 1, 64)
    kbd = np.broadcast_to(kb, (2048, 8, 2, 64)).reshape(2048, 1024)
    gs.append(stat_groups(kbd))
    gs.append(mov_groups(qb[:, 2560:3072]))
    gs.append(stat_groups(w_o_b[0]))
    for h in range(2):
        gs.append(stat_groups(w_up[1][:, h * 4096:(h + 1) * 4096]))
        gs.append(down_groups(w_down[1], h))
    wall = np.concatenate(gs, axis=0)
    assert wall.shape == (NGRP, 128, 4096), wall.shape
    return wall


def rope_tables():
    half = ROT // 2
    freqs = (np.float32(THETA) ** (-np.arange(0, ROT, 2, dtype=np.float32) / np.float32(ROT))).astype(np.float32)
    C = np.ones((9, 128, TT), np.float32)
    S = np.zeros((9, 128, TT), np.float32)
    for t in range(9):
        pos0 = t * TT if t < 8 else 1024
        pos = (pos0 + np.arange(TT)).astype(np.float32)
        ang = (pos[:, None] * freqs[None, :]).astype(np.float32)
        cs = np.cos(ang).astype(np.float32).T
        sn = np.sin(ang).astype(np.float32).T
        for hb in (0, 64):
            C[t, hb:hb + 8] = cs
            C[t, hb + 8:hb + 16] = cs
            S[t, hb:hb + 8] = -sn
            S[t, hb + 8:hb + 16] = sn
    permT = np.zeros((128, 128), np.float32)
    for m in range(128):
        d = m % 64
        if d < 8:
            permT[m + 8, m] = 1.0
        elif d < 16:
            permT[m - 8, m] = 1.0
    return C, S, permT


def build(NT=8, DO_SAMPLE=True):
    nc = bass.Bass("TRN2", target_bir_lowering=False)
    cx = Ctx(nc)

    def din(name, shape, dt=F32):
        return nc.dram_tensor(name, list(shape), dt, kind="ExternalInput")

    def dout(name, shape):
        return nc.dram_tensor(name, list(shape), F32, kind="ExternalOutput")

    xp = din("xp", [4096, 2048])
    xs = din("xs", [16, 2048])
    cT_d = din("cT", [128, 32])
    cak = din("cak", [512, 2048])
    cav = din("cav", [512, 2048])
    cbk = din("cbk", [128, 512])
    cbv = din("cbv", [128, 512])
    wall = din("wall", [NGRP, 128, 4096])
    lnT_d = din("lnT", [128, 128])
    bada_d = din("badaT", [128, 192])
    G_d = din("G", [16, 128, 320])
    tabc_d = din("tabc", [128, 16])
    sink_d = din("sinkT", [128, 16])
    ident_d = din("ident", [128, 128])
    perm_d = din("permT", [128, 128])
    ropeC_d = din("ropeC", [9, 128, TT])
    ropeS_d = din("ropeS", [9, 128, TT])
    wscr_m = nc.dram_tensor("wscr_m", [NMAIN, 128, 4096], BF16, kind="Internal")
    Gscr = nc.dram_tensor("Gscr", [16, 128, 640], BF16, kind="Internal")

    def wscr_g(g):
        assert g >= NADA
        return wscr_m[g - NADA]

    y_p = dout("y_p", [4096, 2048])
    y_s = dout("y_s", [16, 2048])
    ak_p = dout("ak_p", [512, 2048])
    av_p = dout("av_p", [512, 2048])
    bk_p = dout("bk_p", [128, 512])
    bv_p = dout("bv_p", [128, 512])
    ak_s = dout("ak_s", [16, 2048])
    av_s = dout("av_s", [16, 2048])
    bk_s = dout("bk_s", [16, 512])
    bv_s = dout("bv_s", [16, 512])

    def sb(name, shape, dt):
        return nc.alloc_sbuf_tensor("sb_" + name, list(shape), dt)

    xa = sb("xa", [128, 8192], F32)
    R1 = sb("R1", [128, 8192], BF16)
    R2f = sb("R2f", [128, 4096], F32)
    R2 = R2f.bitcast(BF16)
    KA = sb("KA", [128, 16384], BF16)
    VA = sb("VA", [128, 16384], BF16)
    KB = sb("KB", [128, 8 * 640], BF16)
    VB = sb("VB", [128, 5 * 512], BF16)
    wbuf = [sb(f"wbuf{i}", [128, 4096], BF16) for i in range(NBUF)]
    ident = sb("ident", [128, 128], F32)
    permT = sb("permT", [128, 128], F32)
    onesD = sb("onesD", [128, 128], F32)
    onesb = sb("onesb", [128, 128], BF16)
    zb = sb("zb", [128, 512], BF16)
    ropeC = sb("ropeC", [128, TT], F32)
    ropeS = sb("ropeS", [128, TT], F32)
    s1 = sb("s1", [128, TT], F32)
    s2 = sb("s2", [128, TT], F32)
    msb = sb("msb", [128, TT], F32)
    rstd = sb("rstd", [128, TT], F32)
    nmr = sb("nmr", [128, TT], F32)
    rden = nmr
    NWK = 4
    wk = [sb(f"wk{i}", [128, TT], F32) for i in range(NWK)]
    NPT = 6
    LOOK = 5
    PT = [sb(f"PT{i}", [128, TT], BF16) for i in range(NPT)]
    Gf = sb("Gf", [128, 320], F32)
    Ghl = [sb(f"Ghl{i}", [128, 640], BF16) for i in range(2)]
    identb = sb("identb", [128, 128], BF16)
    permTb = sb("permTb", [128, 128], BF16)
    tabc = sb("tabc", [128, 16], F32)
    sinkT = sb("sinkT", [128, 16], F32)
    esink = sb("esink", [128, 16], F32)
    lnT = sb("lnT", [128, 128], F32)
    badaT = sb("badaT", [128, 192], F32)
    cT = sb("cT", [128, 32], F32)
    scT = sb("scT", [128, 32], BF16)
    sig = sb("sig", [128, 32], F32)
    MOD = sb("MOD", [128, 4 * 2 * 48], F32)
    SC = sb("SC", [128, 32 * 16], F32)
    epsT = sb("epsT", [128, 1], F32)

    ps = [nc.alloc_psum_tensor(f"ps{i}", [128, 512], F32) for i in range(8)]

    PE = Eng(nc.tensor, "pe", cx.sem("s_pe"), is_pe=True)
    ACT = Eng(nc.scalar, "act", cx.sem("s_act"))
    DVE = Eng(nc.vector, "dve", cx.sem("s_dve"))
    POOL = Eng(nc.gpsimd, "pool", cx.sem("s_pool"))
    SP = Eng(nc.sync, "sp", None)

    def chan(name):
        return Chan(cx.sem(name), name)

    wchan = [chan(f"c_w{i}") for i in range(NBUF)]
    c_const = chan("c_const")
    c_scr = [chan(f"c_scr{i}") for i in range(NBUF)]
    c_wsw = [chan(f"c_wsw{i}") for i in range(NBUF)]
    c_stg = [chan("c_stg0"), chan("c_stg1")]
    c_G = [chan("c_G0"), chan("c_G1")]
    c_Gf = chan("c_Gf")
    c_Gs = [chan("c_Gs0"), chan("c_Gs1")]
    c_ropeC = chan("c_ropeC")
    c_ropeS = chan("c_ropeS")
    c_out = chan("c_out")
    c_wkout = [chan(f"c_wko{i}") for i in range(NWK)]
    c_cache = chan("c_cache")

    wk_i = [0]

    def next_wk():
        i = wk_i[0] % NWK
        wk_i[0] += 1
        return i

    pt_i = [0]

    def next_pt():
        i = pt_i[0] % NPT
        pt_i[0] += 1
        return i

    bank_lo = [0]

    def next_bank():
        b = bank_lo[0] % 4
        bank_lo[0] += 1
        return b

    def A(t, a, b):
        return t[:, a:b]

    def mm(out, lhsT, rhs, start, stop, reads, writes, signal):
        return cx.op(PE, lambda: nc.tensor.matmul(out, lhsT, rhs, start=start, stop=stop),
                     reads=reads, writes=writes, signal=signal)

    def act(out, in_, func, reads, writes, bias=None, scale=None):
        kw = {}
        if bias is not None:
            kw["bias"] = bias
        if scale is not None:
            kw["scale"] = scale
        return cx.op(ACT, lambda: nc.scalar.activation(out=out, in_=in_, func=func, **kw),
                     reads=reads, writes=writes)

    def tt(E, out, in0, in1, op, reads, writes):
        return cx.op(E, lambda: E.e.tensor_tensor(out=out, in0=in0, in1=in1, op=op),
                     reads=reads, writes=writes)

    def tcopy(E, out, in_, reads, writes):
        return cx.op(E, lambda: E.e.tensor_copy(out=out, in_=in_), reads=reads, writes=writes)

    def stt(out, in0, scalar, in1, op0, op1, reads, writes):
        return cx.op(DVE, lambda: nc.vector.scalar_tensor_tensor(out=out, in0=in0, scalar=scalar, in1=in1,
                                                                 op0=op0, op1=op1),
                     reads=reads, writes=writes)

    def ts(out, in0, s1_, op0, reads, writes, s2_=None, op1=None):
        if op1 is None:
            return cx.op(DVE, lambda: nc.vector.tensor_scalar(out=out, in0=in0, scalar1=s1_, scalar2=None, op0=op0),
                         reads=reads, writes=writes)
        return cx.op(DVE, lambda: nc.vector.tensor_scalar(out=out, in0=in0, scalar1=s1_, scalar2=s2_, op0=op0, op1=op1),
                     reads=reads, writes=writes)

    npass = NT + (1 if DO_SAMPLE else 0)
    main_g = list(range(NADA, NGRP))

    def ada_g(u):
        return list(range(u * 24, (u + 1) * 24))
    order = ada_g(0) + main_g[0:24] + ada_g(1) + main_g[24:32] + ada_g(2) + main_g[32:110] + ada_g(3) + main_g[110:182]
    for _ in range(npass - 1):
        order += main_g
    ws = {"load": 0, "use": 0}
    cast_seq = []
    _seen = set()
    for g in order:
        if g not in _seen:
            _seen.add(g)
            cast_seq.append(g)
    cast_pos = {g: i for i, g in enumerate(cast_seq)}

    def cast_key(g):
        return ("wscr", cast_pos[g] // SG)

    NFIRST = NADA + NMAIN

    def ws_load():
        i = ws["load"]
        if i >= len(order):
            return
        g = order[i]
        b = i % NBUF
        if i < NFIRST:
            cx.dma(POOL, c_wsw[b], out=wbuf[b][:, :].rearrange("p (a b) -> p a b", b=2048),
                   in_=wall[g].rearrange("p (a b) -> p a b", b=2048), writes=[("wbuf", b)])
            if g >= NADA and npass > 1:
                cx.dma(SP, c_scr[b], out=wscr_g(g), in_=wbuf[b][:, :], reads=[("wbuf", b)], writes=[("wscr", g)])
        else:
            cx.dma(SP, wchan[b], out=wbuf[b][:, :], in_=wscr_g(g), reads=[("wscr", g)], writes=[("wbuf", b)])
        ws["load"] += 1

    def ws_use():
        b = ws["use"] % NBUF
        ws["use"] += 1
        return b

    def cload(dst, src, key):
        cx.dma(SP, c_const, out=dst, in_=src, writes=[key])

    cload(ident[:, :], ident_d[:, :], "ident")
    cload(permT[:, :], perm_d[:, :], "permT")
    cload(lnT[:, :], lnT_d[:, :], "lnT")
    cload(badaT[:, :], bada_d[:, :], "badaT")
    cload(cT[:, :], cT_d[:, :], "cT")
    cload(tabc[:, :], tabc_d[:, :], "tabc")
    cload(sinkT[:, :], sink_d[:, :], "sinkT")
    for k_ in ["ident", "permT", "lnT", "badaT", "cT", "tabc", "sinkT"]:
        cx.st[k_] = [(c_const, c_const.cnt), []]

    STOPPED = [False]
    try:
        ckpt(0)
    except StopBuild:
        STOPPED[0] = True
    if not STOPPED[0]:
        for _ in range(NBUF):
            ws_load()

    cx.op(DVE, lambda: nc.vector.memset(onesD[:, :], 1.0 / D), writes=["onesD"])
    cx.op(DVE, lambda: nc.vector.memset(onesb[:, :], 1.0), writes=["onesb"])
    cx.op(DVE, lambda: nc.vector.memset(zb[:, :], 0.0), writes=["zb"])
    cx.op(DVE, lambda: nc.vector.memset(epsT[:, :], EPS), writes=["epsT"])

    act(sig[:, :], cT[:, :], AF.Sigmoid, reads=["cT"], writes=["sig"])
    tt(DVE, scT[:, :], sig[:, :], cT[:, :], ALU.mult, reads=["sig", "cT"], writes=["scT"])
    act(esink[:, :], sinkT[:, :], AF.Exp, reads=["sinkT"], writes=["esink"])

    def modv(u, j, part):
        o = (u * 2 + j) * 48 + part * 16
        return MOD[:, o:o + 16]

    slot_i = [0]

    def new_slot():
        i = slot_i[0]
        slot_i[0] += 1
        return SC[:, i * 16:(i + 1) * 16]

    def lnG(u):
        return lnT[:, u * 16:(u + 1) * 16]

    def lnB(u):
        return lnT[:, 64 + u * 16:64 + (u + 1) * 16]

    OPS = {}
    HA = {}
    HB = {}
    XA = {}
    XB = {}
    A0 = {}
    for u in range(4):
        for j in range(2):
            OPS[(u, j)] = new_slot()
            if u >= 1:
                HA[(u, j)] = new_slot()
                HB[(u, j)] = new_slot()
        XA[u] = new_slot()
        XB[u] = new_slot()
    for j in range(2):
        A0[j] = new_slot()
    for u in range(4):
        f = ALPHA if u < 3 else 1.0
        ts(XA[u], lnG(u), f, ALU.mult, reads=["lnT"], writes=[("XA", u)])
        ts(XB[u], lnB(u), f, ALU.mult, reads=["lnT"], writes=[("XB", u)])

    def do_ada(u):
        bank = next_bank()
        for g in range(24):
            b = ws_use()
            for jj in range(2):
                chunk = 2 * g + jj
                for kc in range(KC):
                    last = (kc == KC - 1)
                    mm(ps[bank][:, chunk * 2:chunk * 2 + 2],
                       wbuf[b][:, (jj * 16 + kc) * 128:(jj * 16 + kc + 1) * 128],
                       scT[:, kc * 2:kc * 2 + 2], kc == 0, last,
                       reads=[("wbuf", b), "scT"], writes=[("ps", bank)], signal=last)
            ws_load()
        for j in range(2):
            o = (u * 2 + j) * 48
            src = ps[bank][:, 0:96].rearrange("p (c j) -> p c j", j=2)[:, :, j]
            tt(DVE, MOD[:, o:o + 48], src, badaT[:, u * 48:(u + 1) * 48], ALU.add,
               reads=[("ps", bank), "badaT"], writes=[("MOD", u, j)])
            ts(OPS[(u, j)], modv(u, j, 1), 1.0, ALU.add, reads=[("MOD", u, j)], writes=[("OPS", u, j)])
            if u == 0:
                ts(A0[j], OPS[(0, j)], 1.0 / ALPHA, ALU.mult, reads=[("OPS", 0, j)], writes=[("A0", j)])
            else:
                tt(DVE, HA[(u, j)], lnG(u - 1), OPS[(u, j)], ALU.mult, reads=["lnT", ("OPS", u, j)], writes=[("HA", u, j)])
                tt(DVE, HB[(u, j)], lnB(u - 1), OPS[(u, j)], ALU.mult, reads=["lnT", ("OPS", u, j)], writes=[("HB", u, j)])
                tt(DVE, HB[(u, j)], HB[(u, j)], modv(u, j, 0), ALU.add, reads=[("HB", u, j), ("MOD", u, j)],
                   writes=[("HB", u, j)])

    tcopy(DVE, identb[:, :], ident[:, :], reads=["ident"], writes=["identb"])
    tcopy(DVE, permTb[:, :], permT[:, :], reads=["permT"], writes=["permTb"])
    for h in range(16):
        gi = h % 2
        cx.dma(SP, c_Gf, out=Gf[:, :], in_=G_d[h], writes=["Gf"])
        act(Ghl[gi][:, 0:320], Gf[:, :], AF.Identity, reads=["Gf"], writes=[("G", gi)])
        w = next_wk()
        tt(DVE, wk[w][:, 0:320], Gf[:, :], Ghl[gi][:, 0:320], ALU.subtract, reads=["Gf", ("G", gi)], writes=[("wk", w)])
        act(Ghl[gi][:, 320:640], wk[w][:, 0:320], AF.Identity, reads=[("wk", w)], writes=[("G", gi)])
        cx.dma(SP, c_Gs[gi], out=Gscr[h], in_=Ghl[gi][:, :], reads=[("G", gi)], writes=[("Gscr", h)])

    do_ada(0)

    def xa_c(k, T):
        return xa[:, k * TT:k * TT + T]

    def r1_c(k, T):
        return R1[:, k * TT:k * TT + T]

    def q_c(k, T):
        return R2[:, k * TT:k * TT + T]

    def stg(j):
        return R2f[:, j * 2048:(j + 1) * 2048]

    def stg_keys(j):
        return [("q", k) for k in range(8 * j, 8 * j + 8)]

    def hid_c(hc, T):
        if hc < 16:
            return KA[:, 8192 + hc * TT:8192 + hc * TT + T]
        return VA[:, 8192 + (hc - 16) * TT:8192 + (hc - 16) * TT + T]

    def hid_key(hc):
        if hc < 16:
            return ("KA", 1, hc)
        return ("VA", 1, (hc - 16) // 4)

    def proj_stat(ngroups, rhs_fn, rhs_keys, T, evac, kcs=16, chunks_per_group=2):
        for g in range(ngroups):
            b = ws_use()
            for jj in range(chunks_per_group):
                chunk = chunks_per_group * g + jj
                bank = next_bank()
                for kc in range(kcs):
                    last = (kc == kcs - 1)
                    mm(ps[bank][:, 0:T], wbuf[b][:, (jj * kcs + kc) * 128:(jj * kcs + kc + 1) * 128], rhs_fn(kc),
                       kc == 0, last, reads=[("wbuf", b), rhs_keys(kc)], writes=[("ps", bank)], signal=last)
                evac(chunk, bank)
            ws_load()

    def proj_mov(ngroups, lhs_fn, lhs_keys, ntb, tbw, evac):
        for g in range(ngroups):
            b = ws_use()
            for tb in range(ntb):
                bank = next_bank()
                for kc in range(KC):
                    last = (kc == KC - 1)
                    mm(ps[bank][0:tbw, 0:256], lhs_fn(kc, tb), wbuf[b][:, kc * 256:(kc + 1) * 256],
                       kc == 0, last, reads=[("wbuf", b), lhs_keys(kc)], writes=[("ps", bank)], signal=last)
                evac(g, tb, bank)
            ws_load()

    def out_small(dst_ap, src_ap, wki):
        cx.dma(SP, c_wkout[wki], out=dst_ap, in_=src_ap, reads=[("wk", wki)], writes=[])

    def resid_and_stats(n, bank, T, gate_ap, j, u, do_stats):
        stt(xa_c(n, T), ps[bank][:, 0:T], gate_ap[:, n:n + 1], xa_c(n, T), ALU.mult, ALU.add,
            reads=[("ps", bank), ("xa", n), ("MOD", u, j)], writes=[("xa", n)])
        if do_stats:
            if n == 0:
                act(s2[:, 0:T], xa_c(n, T), AF.Square, reads=[("xa", n)], writes=["s2"])
                tcopy(DVE, s1[:, 0:T], xa_c(n, T), reads=[("xa", n)], writes=["s1"])
            else:
                w = next_wk()
                act(wk[w][:, 0:T], xa_c(n, T), AF.Square, reads=[("xa", n)], writes=[("wk", w)])
                tt(PL[0], s1[:, 0:T], s1[:, 0:T], xa_c(n, T), ALU.add, reads=["s1", ("xa", n)], writes=["s1"])
                tt(PL[0], s2[:, 0:T], s2[:, 0:T], wk[w][:, 0:T], ALU.add, reads=["s2", ("wk", w)], writes=["s2"])

    def ln_finish(T, u, j, final):
        b1 = next_bank()
        mm(ps[b1][:, 0:T], onesD[:, :], s1[:, 0:T], True, True, reads=["onesD", "s1"], writes=[("ps", b1)], signal=True)
        b2 = next_bank()
        mm(ps[b2][:, 0:T], onesD[:, :], s2[:, 0:T], True, True, reads=["onesD", "s2"], writes=[("ps", b2)], signal=True)
        tcopy(DVE, msb[:, 0:T], ps[b1][:, 0:T], reads=[("ps", b1)], writes=["msb"])
        tt(DVE, s1[:, 0:T], msb[:, 0:T], msb[:, 0:T], ALU.mult, reads=["msb"], writes=["s1"])
        tt(DVE, s1[:, 0:T], ps[b2][:, 0:T], s1[:, 0:T], ALU.subtract, reads=[("ps", b2), "s1"], writes=["s1"])
        act(s2[:, 0:T], s1[:, 0:T], AF.Ln, reads=["s1", "epsT"], writes=["s2"], bias=epsT[:, 0:1])
        act(rstd[:, 0:T], s2[:, 0:T], AF.Exp, reads=["s2"], writes=["rstd"], scale=-0.5)
        stt(nmr[:, 0:T], msb[:, 0:T], -1.0, rstd[:, 0:T], ALU.mult, ALU.mult, reads=["msb", "rstd"], writes=["nmr"])
        for k in range(KC):
            w = next_wk()
            tt(DVE, wk[w][:, 0:T], xa_c(k, T), rstd[:, 0:T], ALU.mult, reads=[("xa", k), "rstd"], writes=[("wk", w)])
            tt(PL[0], wk[w][:, 0:T], wk[w][:, 0:T], nmr[:, 0:T], ALU.add, reads=[("wk", w), "nmr"], writes=[("wk", w)])
            act(xa_c(k, T), wk[w][:, 0:T], AF.Identity, reads=[("wk", w), ("XA", u), ("XB", u)], writes=[("xa", k)],
                scale=XA[u][:, k:k + 1], bias=XB[u][:, k:k + 1])
            if not final:
                act(r1_c(k, T), wk[w][:, 0:T], AF.Identity, reads=[("wk", w), ("HA", u + 1, j), ("HB", u + 1, j)],
                    writes=[("R1", k)], scale=HA[(u + 1, j)][:, k:k + 1], bias=HB[(u + 1, j)][:, k:k + 1])

    def mlp(T, u, j, final):
        gate_ap = modv(u, j, 2)
        for half in range(2):
            def evac_up(chunk, bank):
                w = next_wk()
                act(wk[w][:, 0:T], ps[bank][:, 0:T], AF.Relu, reads=[("ps", bank)], writes=[("wk", w)])
                tt(DVE, hid_c(chunk, T), wk[w][:, 0:T], wk[w][:, 0:T], ALU.mult, reads=[("wk", w)],
                   writes=[hid_key(chunk)])
            proj_stat(16, lambda kc: r1_c(kc, T), lambda kc: ("R1", kc), T, evac_up)

            def evac_dn(n, bank):
                resid_and_stats(n, bank, T, gate_ap, j, u, do_stats=(half == 1))
            proj_stat(16, lambda kc: hid_c(kc, T), lambda kc: hid_key(kc), T, evac_dn, kcs=32, chunks_per_group=1)
        ln_finish(T, u, j, final)

    def oproj(T, u, j):
        gate_ap = modv(u, j, 2)

        def evac(n, bank):
            resid_and_stats(n, bank, T, gate_ap, j, u, do_stats=True)
        proj_stat(8, lambda kc: r1_c(kc, T), lambda kc: ("R1", kc), T, evac)
        ln_finish(T, u, j, False)

    def out_transposed(src_fn, src_keys, nfc, tbw, dst_fn, width_cols=128):
        for fc0 in range(0, nfc, 4):
            nf = min(4, nfc - fc0)
            bank = next_bank()
            for i in range(nf):
                cx.op(PE, lambda i=i: nc.tensor.transpose(ps[bank][0:tbw, i * 128:(i + 1) * 128], src_fn(fc0 + i), ident[:, :]),
                      reads=[src_keys(fc0 + i), "ident"], writes=[("ps", bank)], signal=(i == nf - 1))
            w = next_wk()
            act(wk[w][0:tbw, 0:nf * 128], ps[bank][0:tbw, 0:nf * 128], AF.Identity, reads=[("ps", bank)], writes=[("wk", w)])
            dst_fn(fc0, nf, w)

    PL = [DVE]
    PREF = set()
    def run_pass(pi):
        PASS_MARK.append((pi, getattr(PE, 'nops', 0)))
        is_sample = (pi == 8)
        PL[0] = DVE if pi == 0 else POOL
        j = 1 if is_sample else 0
        T = 16 if is_sample else TT
        ntb = 1 if is_sample else 4
        tbw = 16 if is_sample else 128
        nqc = 1 if is_sample else 8
        qcw = 16 if is_sample else 64
        has_hist = is_sample or pi > 0
        is_last = is_sample or pi == 7
        xsrc = xs if is_sample else xp
        row0 = 0 if is_sample else pi * TT

        pre = pi in PREF
        if not pre:
            cx.dma(SP, c_ropeC, out=ropeC[:, :], in_=ropeC_d[pi], writes=["ropeC"])
            cx.dma(SP, c_ropeS, out=ropeS[:, :], in_=ropeS_d[pi], writes=["ropeS"])

        if is_sample:
            for blk in range(4):
                cx.dma(POOL, c_cache, out=VA[:, blk * 2048:(blk + 1) * 2048], in_=cav[blk * 128:(blk + 1) * 128, :],
                       writes=[("VA", 0, blk)])
            cx.dma(POOL, c_cache, out=VB[:, 0:512], in_=cbv[:, :], writes=[("VB", 0)])
            for k_ in [("VA", 0, 0), ("VA", 0, 1), ("VA", 0, 2), ("VA", 0, 3), ("VB", 0)]:
                cx.st[k_] = [(c_cache, c_cache.cnt), []]
            for blk in range(4):
                s = blk % 2
                cx.dma(SP, c_stg[s], out=stg(s), in_=cak[blk * 128:(blk + 1) * 128, :], writes=stg_keys(s))
                for h0 in range(0, 16, 4):
                    bank = next_bank()
                    for i in range(4):
                        h = h0 + i
                        cx.op(PE, lambda i=i, h=h, s=s: nc.tensor.transpose(ps[bank][:, i * 128:(i + 1) * 128],
                                                                             stg(s)[:, h * 128:(h + 1) * 128], ident[:, :]),
                              reads=stg_keys(s) + ["ident"], writes=[("ps", bank)], signal=(i == 3))
                    dst = KA[:, 0:8192].rearrange("p (h t) -> p h t", t=TT)[:, h0:h0 + 4, blk * 128:(blk + 1) * 128]
                    src = ps[bank][:, :].rearrange("p (h t) -> p h t", t=128)
                    cx.op(ACT, lambda dst=dst, src=src: nc.scalar.activation(out=dst, in_=src, func=AF.Identity),
                          reads=[("ps", bank)], writes=[("KA", 0, h0 + i) for i in range(4)])
            s = 0
            stgv = stg(s)[:, 0:1024].rearrange("p (g r d) -> p g r d", r=2, d=64)
            srcv = cbk[:, :].rearrange("p (g d) -> p g d", d=64)
            for r in range(2):
                cx.dma(SP, c_stg[s], out=stgv[:, :, r, :], in_=srcv, writes=stg_keys(s))
            for g0 in range(0, 8, 4):
                bank = next_bank()
                for i in range(4):
                    g = g0 + i
                    cx.op(PE, lambda i=i, g=g: nc.tensor.transpose(ps[bank][:, i * 128:(i + 1) * 128],
                                                                    stg(0)[:, g * 128:(g + 1) * 128], ident[:, :]),
                          reads=stg_keys(0) + ["ident"], writes=[("ps", bank)], signal=(i == 3))
                dst = KB[:, :].rearrange("p (g t) -> p g t", t=640)[:, g0:g0 + 4, 0:128]
                src = ps[bank][:, :].rearrange("p (g t) -> p g t", t=128)
                cx.op(ACT, lambda dst=dst, src=src: nc.scalar.activation(out=dst, in_=src, func=AF.Identity),
                      reads=[("ps", bank)], writes=[("KB", g0 + i) for i in range(4)])

        for tb in range(ntb):
            s = tb % 2
            if not (pre and tb < 2):
                cx.dma(SP, c_stg[s], out=stg(s)[0:tbw, :], in_=xsrc[row0 + tb * tbw:row0 + (tb + 1) * tbw, :],
                       writes=stg_keys(s))
            for k0 in range(0, KC, 4):
                bank = next_bank()
                for i in range(4):
                    k = k0 + i
                    cx.op(PE, lambda i=i, k=k, s=s: nc.tensor.transpose(ps[bank][:, i * 128:i * 128 + tbw],
                                                                         stg(s)[0:tbw, k * 128:(k + 1) * 128],
                                                                         ident[0:tbw, 0:tbw]),
                          reads=stg_keys(s) + ["ident"], writes=[("ps", bank)], signal=(i == 3))
                for i in range(4):
                    k = k0 + i
                    src = ps[bank][:, i * 128:i * 128 + tbw]
                    dsl = slice(k * TT + tb * tbw, k * TT + (tb + 1) * tbw)
                    cx.op(DVE, lambda dsl=dsl, src=src: nc.vector.tensor_scalar(
                        out=xa[:, dsl], in0=src, scalar1=ALPHA, scalar2=None, op0=ALU.mult),
                        reads=[("ps", bank)], writes=[("xa", k)])
                    cx.op(ACT, lambda k=k, dsl=dsl: nc.scalar.activation(
                        out=R1[:, dsl], in_=xa[:, dsl], func=AF.Identity,
                        scale=A0[j][:, k:k + 1], bias=modv(0, j, 0)[:, k:k + 1]),
                        reads=[("xa", k), ("A0", j), ("MOD", 0, j)], writes=[("R1", k)])

        ckpt(4)
        scaleA = float(128 ** -0.5)

        def evac_qa(h, bank):
            act(q_c(h, T), ps[bank][:, 0:T], AF.Identity, reads=[("ps", bank)], writes=[("q", h)], scale=scaleA)
        proj_stat(8, lambda kc: r1_c(kc, T), lambda kc: ("R1", kc), T, evac_qa)

        def evac_ka(h, bank):
            tcopy(DVE, KA[:, 8192 + h * TT:8192 + h * TT + T], ps[bank][:, 0:T], reads=[("ps", bank)],
                  writes=[("KA", 1, h)])
            if is_last:
                w = next_wk()
                tcopy(DVE, wk[w][:, 0:T], ps[bank][:, 0:T], reads=[("ps", bank)], writes=[("wk", w)])
                dsto = ak_s if is_sample else ak_p
                bk2 = next_bank()
                for tb in range(ntb):
                    cx.op(PE, lambda tb=tb, w=w: nc.tensor.transpose(ps[bk2][0:tbw, tb * 128:(tb + 1) * 128],
                                                                      wk[w][:, tb * tbw:(tb + 1) * tbw], ident[:, :]),
                          reads=[("wk", w), "ident"], writes=[("ps", bk2)], signal=(tb == ntb - 1))
                w2 = next_wk()
                act(wk[w2][0:tbw, 0:ntb * 128], ps[bk2][0:tbw, 0:ntb * 128], AF.Identity, reads=[("ps", bk2)],
                    writes=[("wk", w2)])
                dst = dsto[:, h * 128:(h + 1) * 128].rearrange("(tb p) d -> p tb d", p=tbw)
                src = wk[w2][0:tbw, 0:ntb * 128].rearrange("p (tb d) -> p tb d", d=128)
                out_small(dst, src, w2)
        proj_stat(8, lambda kc: r1_c(kc, T), lambda kc: ("R1", kc), T, evac_ka)

        def evac_va(g, tb, bank):
            tcopy(DVE, VA[0:tbw, 8192 + tb * 2048 + g * 256:8192 + tb * 2048 + (g + 1) * 256], ps[bank][0:tbw, 0:256],
                  reads=[("ps", bank)], writes=[("VA", 1, tb)])
            if is_last:
                w = next_wk()
                tcopy(DVE, wk[w][0:tbw, 0:256], ps[bank][0:tbw, 0:256], reads=[("ps", bank)], writes=[("wk", w)])
                dsto = av_s if is_sample else av_p
                out_small(dsto[tb * tbw:(tb + 1) * tbw, g * 256:(g + 1) * 256], wk[w][0:tbw, 0:256], w)
        proj_mov(8, lambda kc, tb: R1[:, kc * TT + tb * tbw:kc * TT + (tb + 1) * tbw], lambda kc: ("R1", kc), ntb, tbw,
                 evac_va)

        if pi == 0:
            do_ada(1)
        ckpt(5)
        kbs = []
        if has_hist:
            for i in range(4):
                kbs.append((0, i, 128, 2 * i, True))
        for i in range(ntb):
            kbs.append((1, i, tbw, 8 + 2 * i, not is_sample))
        pend = []

        def flush_pend():
            while pend:
                pend.pop(0)()

        for h in range(16):
            gi = h % 2
            cx.dma(SP, c_G[gi], out=Ghl[gi][:, :], in_=Gscr[h], reads=[("Gscr", h)], writes=[("G", gi)])
            pvb = 4 + 2 * (h % 2)
            dnb = pvb + 1
            mm(ps[pvb][:, 0:T], zb[:, 0:128], zb[:, 0:T], True, False, reads=["zb"], writes=[("ps", pvb)], signal=False)
            mm(ps[dnb][:, 0:T], zb[:, 0:128], zb[:, 0:T], True, False, reads=["zb"], writes=[("ps", dnb)], signal=False)
            nk_list = []
            for (slot, blk, nk, e0, two) in kbs:
                c_lo = max(0, e0 - 8)
                c_hi = min(nqc - 1, e0 + 1 if two else e0)
                if c_hi < c_lo:
                    continue
                nk_list.append((slot, blk, nk, e0, two, c_lo, c_hi))
            for idx, (slot, blk, nk, e0, two, c_lo, c_hi) in enumerate(nk_list):
                lastkb = (idx == len(nk_list) - 1)
                ncols = (c_hi - c_lo + 1) * qcw
                q0 = c_lo * qcw
                sb_ = next_bank()
                m0 = 64 * (9 - e0) + q0
                nn = max(0, min(ncols, 320 - m0))
                mm(ps[sb_][0:nk, 0:ncols], KA[:, slot * 8192 + h * TT + blk * 128:slot * 8192 + h * TT + blk * 128 + nk],
                   R2[:, h * TT + q0:h * TT + q0 + ncols], True, nn == 0,
                   reads=[("KA", slot, h), ("q", h)], writes=[("ps", sb_)], signal=(nn == 0))
                if nn > 0:
                    mm(ps[sb_][0:nk, 0:nn], identb[0:nk, 0:nk], Ghl[gi][0:nk, m0:m0 + nn], False, False,
                       reads=["identb", ("G", gi)], writes=[("ps", sb_)], signal=False)
                    mm(ps[sb_][0:nk, 0:nn], identb[0:nk, 0:nk], Ghl[gi][0:nk, 320 + m0:320 + m0 + nn], False, True,
                       reads=["identb", ("G", gi)], writes=[("ps", sb_)], signal=True)
                p = next_pt()
                if nn > 0:
                    act(PT[p][0:nk, 0:nn], ps[sb_][0:nk, 0:nn], AF.Exp, reads=[("ps", sb_)], writes=[("PT", p)])
                if nn < ncols:
                    act(PT[p][0:nk, nn:ncols], ps[sb_][0:nk, nn:ncols], AF.Exp, reads=[("ps", sb_), "tabc"],
                        writes=[("PT", p)], bias=tabc[0:nk, h:h + 1])
                if two:
                    c = e0 + 1
                    if c_lo <= c <= c_hi:
                        cx.op(PL[0], lambda p=p, c=c: PL[0].e.memset(PT[p][0:64, (c - c_lo) * qcw:(c - c_lo + 1) * qcw], 0.0),
                              writes=[("PT", p)])
                    c = e0 - 8
                    if c_lo <= c <= c_hi:
                        cx.op(PL[0], lambda p=p, c=c: PL[0].e.memset(PT[p][64:128, (c - c_lo) * qcw:(c - c_lo + 1) * qcw], 0.0),
                              writes=[("PT", p)])

                def pv(slot=slot, blk=blk, nk=nk, p=p, q0=q0, ncols=ncols, lastkb=lastkb, pvb=pvb, dnb=dnb, h=h):
                    mm(ps[pvb][:, q0:q0 + ncols], VA[0:nk, slot * 8192 + blk * 2048 + h * 128:slot * 8192 + blk * 2048 + (h + 1) * 128],
                       PT[p][0:nk, 0:ncols], False, lastkb, reads=[("VA", slot, blk), ("PT", p)], writes=[("ps", pvb)],
                       signal=lastkb)
                    mm(ps[dnb][:, q0:q0 + ncols], onesb[0:nk, :], PT[p][0:nk, 0:ncols], False, lastkb,
                       reads=["onesb", ("PT", p)], writes=[("ps", dnb)], signal=lastkb)
                    if lastkb:
                        wl = next_wk()
                        act(wk[wl][:, 0:T], ps[dnb][:, 0:T], AF.Ln, reads=[("ps", dnb)], writes=[("wk", wl)])
                        act(rden[:, 0:T], wk[wl][:, 0:T], AF.Exp, reads=[("wk", wl)], writes=["nmr"], scale=-1.0)
                        tt(DVE, r1_c(h, T), ps[pvb][:, 0:T], rden[:, 0:T], ALU.mult, reads=[("ps", pvb), "nmr"],
                           writes=[("R1", h)])
                pend.append(pv)
                if len(pend) > min(LOOK, len(kbs) - 1):
                    pend.pop(0)()
        flush_pend()

        if not is_sample and pi < NT - 1:
            for h in range(16):
                pass
            cx.op(DVE, lambda: nc.vector.tensor_copy(out=KA[:, 0:8192], in_=KA[:, 8192:16384]),
                  reads=[("KA", 1, h) for h in range(16)], writes=[("KA", 0, h) for h in range(16)])
            cx.op(DVE, lambda: nc.vector.tensor_copy(out=VA[:, 0:8192], in_=VA[:, 8192:16384]),
                  reads=[("VA", 1, b) for b in range(4)], writes=[("VA", 0, b) for b in range(4)])

        ckpt(6)
        oproj(T, 0, j)
        if pi == 0:
            do_ada(2)
        ckpt(7)
        mlp(T, 1, j, False)
        ckpt(8)

        def rope_evac(dst_ap, dst_key, bank, qscale, keep_f32=None):
            w0 = next_wk()
            act(wk[w0][:, 0:T], ps[bank][:, 0:T], AF.Identity, reads=[("ps", bank)], writes=[("wk", w0)], scale=qscale)
            ph = next_pt()
            act(PT[ph][:, 0:T], ps[bank][:, 0:T], AF.Identity, reads=[("ps", bank)], writes=[("PT", ph)], scale=qscale)
            b2 = next_bank()
            mm(ps[b2][:, 0:T], permTb[:, :], PT[ph][:, 0:T], True, True, reads=["permTb", ("PT", ph)], writes=[("ps", b2)],
               signal=True)
            w1 = next_wk()
            tt(DVE, wk[w1][:, 0:T], ps[b2][:, 0:T], ropeS[:, 0:T], ALU.mult, reads=[("ps", b2), "ropeS"], writes=[("wk", w1)])
            tt(PL[0], wk[w0][:, 0:T], wk[w0][:, 0:T], ropeC[:, 0:T], ALU.mult, reads=[("wk", w0), "ropeC"], writes=[("wk", w0)])
            if keep_f32 is None:
                tt(DVE, dst_ap, wk[w0][:, 0:T], wk[w1][:, 0:T], ALU.add, reads=[("wk", w0), ("wk", w1)], writes=[dst_key])
            else:
                tt(DVE, wk[w0][:, 0:T], wk[w0][:, 0:T], wk[w1][:, 0:T], ALU.add, reads=[("wk", w0), ("wk", w1)],
                   writes=[("wk", w0)])
                tcopy(PL[0], dst_ap, wk[w0][:, 0:T], reads=[("wk", w0)], writes=[dst_key])
                keep_f32(w0)

        def evac_qb(i, bank):
            rope_evac(q_c(i, T), ("q", i), bank, 0.125)
        proj_stat(8, lambda kc: r1_c(kc, T), lambda kc: ("R1", kc), T, evac_qb)

        def evac_kb(g, bank):
            dst = KB[:, g * 640 + 128:g * 640 + 128 + T]
            if is_last:
                def keep(w0):
                    t0 = T - tbw
                    bk2 = next_bank()
                    cx.op(PE, lambda: nc.tensor.transpose(ps[bk2][0:tbw, 0:128], wk[w0][:, t0:t0 + tbw], ident[:, :]),
                          reads=[("wk", w0), "ident"], writes=[("ps", bk2)], signal=True)
                    w2 = next_wk()
                    act(wk[w2][0:tbw, 0:64], ps[bk2][0:tbw, 0:64], AF.Identity, reads=[("ps", bk2)], writes=[("wk", w2)])
                    dsto = bk_s if is_sample else bk_p
                    out_small(dsto[:, g * 64:(g + 1) * 64], wk[w2][0:tbw, 0:64], w2)
                rope_evac(dst, ("KB", g), bank, 1.0, keep_f32=keep)
            else:
                rope_evac(dst, ("KB", g), bank, 1.0)
        proj_stat(4, lambda kc: r1_c(kc, T), lambda kc: ("R1", kc), T, evac_kb)

        def evac_vb(g, tb, bank):
            tcopy(DVE, VB[0:tbw, (1 + tb) * 512 + g * 256:(1 + tb) * 512 + (g + 1) * 256], ps[bank][0:tbw, 0:256],
                  reads=[("ps", bank)], writes=[("VB", 1 + tb)])
            if is_last and tb == ntb - 1:
                w = next_wk()
                tcopy(DVE, wk[w][0:tbw, 0:256], ps[bank][0:tbw, 0:256], reads=[("ps", bank)], writes=[("wk", w)])
                dsto = bv_s if is_sample else bv_p
                out_small(dsto[:, g * 256:(g + 1) * 256], wk[w][0:tbw, 0:256], w)
        proj_mov(2, lambda kc, tb: R1[:, kc * TT + tb * tbw:kc * TT + (tb + 1) * tbw], lambda kc: ("R1", kc), ntb, tbw,
                 evac_vb)

        if pi == 0:
            do_ada(3)
        ckpt(9)
        kbsB = []
        if has_hist:
            kbsB.append((0, 0, 128, 0, True))
        for i in range(ntb):
            kbsB.append((128 + i * 128, 1 + i, tbw, 2 + 2 * i, not is_sample))
        for pr in range(16):
            pvb = 4 + 2 * (pr % 2)
            dnb = pvb + 1
            g = pr // 2
            mm(ps[pvb][:, 0:T], zb[:, 0:128], zb[:, 0:T], True, False, reads=["zb"], writes=[("ps", pvb)], signal=False)
            mm(ps[dnb][:, 0:T], zb[:, 0:128], zb[:, 0:T], True, False, reads=["zb"], writes=[("ps", dnb)], signal=False)
            items = []
            for hf in range(2):
                for (col0, vblk, nk, e0, two) in kbsB:
                    c_lo = max(0, e0 - 2)
                    c_hi = min(nqc - 1, e0 + 1 if two else e0)
                    if c_hi < c_lo:
                        continue
                    items.append((hf, col0, vblk, nk, e0, two, c_lo, c_hi))
            for idx, (hf, col0, vblk, nk, e0, two, c_lo, c_hi) in enumerate(items):
                lastit = (idx == len(items) - 1)
                lasthalf = lastit or (items[idx + 1][0] != hf)
                ncols = (c_hi - c_lo + 1) * qcw
                q0 = c_lo * qcw
                p0 = 64 * hf
                sb_ = next_bank()
                mm(ps[sb_][0:nk, 0:ncols], KB[p0:p0 + 64, g * 640 + col0:g * 640 + col0 + nk],
                   R2[p0:p0 + 64, pr * TT + q0:pr * TT + q0 + ncols], True, True,
                   reads=[("KB", g), ("q", pr)], writes=[("ps", sb_)], signal=True)
                p = next_pt()
                act(PT[p][0:nk, 0:ncols], ps[sb_][0:nk, 0:ncols], AF.Exp, reads=[("ps", sb_)], writes=[("PT", p)])
                if two:
                    c = e0 + 1
                    if c_lo <= c <= c_hi:
                        cx.op(PL[0], lambda p=p, c=c, c_lo=c_lo: PL[0].e.memset(
                            PT[p][0:64, (c - c_lo) * qcw:(c - c_lo + 1) * qcw], 0.0), writes=[("PT", p)])
                    c = e0 - 2
                    if c_lo <= c <= c_hi:
                        cx.op(PL[0], lambda p=p, c=c, c_lo=c_lo: PL[0].e.memset(
                            PT[p][64:128, (c - c_lo) * qcw:(c - c_lo + 1) * qcw], 0.0), writes=[("PT", p)])

                def pvB(vblk=vblk, nk=nk, p=p, q0=q0, ncols=ncols, lastit=lastit, pvb=pvb, dnb=dnb, g=g, p0=p0, pr=pr,
                        lasthalf=lasthalf):
                    mm(ps[pvb][p0:p0 + 64, q0:q0 + ncols], VB[0:nk, vblk * 512 + g * 64:vblk * 512 + (g + 1) * 64],
                       PT[p][0:nk, 0:ncols], False, lasthalf, reads=[("VB", vblk), ("PT", p)], writes=[("ps", pvb)],
                       signal=lastit)
                    mm(ps[dnb][p0:p0 + 64, q0:q0 + ncols], onesb[0:nk, 0:64], PT[p][0:nk, 0:ncols], False, lasthalf,
                       reads=["onesb", ("PT", p)], writes=[("ps", dnb)], signal=lastit)
                    if lastit:
                        wl = next_wk()
                        act(wk[wl][:, 0:T], ps[dnb][:, 0:T], AF.Ln, reads=[("ps", dnb), "esink"], writes=[("wk", wl)],
                            bias=esink[:, pr:pr + 1])
                        act(rden[:, 0:T], wk[wl][:, 0:T], AF.Exp, reads=[("wk", wl)], writes=["nmr"], scale=-1.0)
                        tt(DVE, r1_c(pr, T), ps[pvb][:, 0:T], rden[:, 0:T], ALU.mult, reads=[("ps", pvb), "nmr"],
                           writes=[("R1", pr)])
                pend.append(pvB)
                if len(pend) > min(LOOK, 2 * len(kbsB) - 1):
                    pend.pop(0)()
        flush_pend()

        if not is_sample and pi < NT - 1:
            cx.dma(SP, c_ropeC, out=ropeC[:, :], in_=ropeC_d[pi + 1], writes=["ropeC"])
            cx.dma(SP, c_ropeS, out=ropeS[:, :], in_=ropeS_d[pi + 1], writes=["ropeS"])
            for tb_ in range(2):
                cx.dma(SP, c_stg[tb_], out=stg(tb_)[0:128, :],
                       in_=xp[(pi + 1) * TT + tb_ * 128:(pi + 1) * TT + (tb_ + 1) * 128, :], writes=stg_keys(tb_))
            PREF.add(pi + 1)
        if not is_sample and pi < NT - 1:
            kbv = KB[:, :].rearrange("p (g t) -> p g t", t=640)
            cx.op(DVE, lambda: nc.vector.tensor_copy(out=kbv[:, :, 0:128], in_=kbv[:, :, 512:640]),
                  reads=[("KB", g) for g in range(8)], writes=[("KB", g) for g in range(8)])
            cx.op(DVE, lambda: nc.vector.tensor_copy(out=VB[:, 0:512], in_=VB[:, 2048:2560]),
                  reads=[("VB", 4)], writes=[("VB", 0)])

        ckpt(10)
        oproj(T, 2, j)
        mlp(T, 3, j, True)
        ckpt(11)

        ydst = y_s if is_sample else y_p
        for tb in range(ntb):
            def dstf(fc0, nf, w, tb=tb):
                out_small(ydst[row0 + tb * tbw:row0 + (tb + 1) * tbw, fc0 * 128:(fc0 + nf) * 128], wk[w][0:tbw, 0:nf * 128], w)
            out_transposed(lambda fc, tb=tb: xa[:, fc * TT + tb * tbw:fc * TT + (tb + 1) * tbw], lambda fc: ("xa", fc),
                           16, tbw, dstf)

    try:
        ckpt(3)
        for pi in range(NT):
            run_pass(pi)
        if DO_SAMPLE:
            run_pass(8)
    except StopBuild:
        pass

    for ch in c_wkout:
        if ch.cnt:
            nc.sync.wait_ge(ch.sem, ch.cnt)
    return nc


_CACHE = {}


def kernel(x_prompt, x_sample, cache_a_k, cache_a_v, cache_b_k, cache_b_v, c_prompt, c_sample,
           w_ada, b_ada, ln_g, ln_b, w_qkv_a, w_o_a, rel_bias_a, w_qkv_b, w_o_b, sink_b, w_up, w_down,
           _NT=8, _DO_SAMPLE=True, _cores=8):
    f = lambda a: np.ascontiguousarray(np.asarray(a, dtype=np.float32))
    x_prompt, x_sample = f(x_prompt), f(x_sample)
    wall = build_weights(f(w_ada), f(w_qkv_a), f(w_o_a), f(w_qkv_b), f(w_o_b), f(w_up), f(w_down))
    ropeC, ropeS, permT = rope_tables()
    ident = np.eye(128, dtype=np.float32)
    lg = f(ln_g).reshape(4, 16, 128).transpose(2, 0, 1)
    lb = f(ln_b).reshape(4, 16, 128).transpose(2, 0, 1)
    lnT = np.ascontiguousarray(np.concatenate([lg.reshape(128, 64), lb.reshape(128, 64)], axis=1))
    badaT = np.ascontiguousarray(f(b_ada).reshape(4, 48, 128).transpose(2, 0, 1).reshape(128, 192))
    tab = f(rel_bias_a)[0]
    pp = np.arange(128)[:, None]
    mmi = np.arange(320)[None, :]
    idx = np.clip(mmi - pp - 64, -128, 128) + 128
    G = np.ascontiguousarray(tab[:, idx])
    tabc = np.ascontiguousarray(np.broadcast_to(tab[:, 256][None, :], (128, 16)))
    sk = f(sink_b)[0]
    sinkT = np.ascontiguousarray(np.stack([np.broadcast_to(sk[0::2][None], (64, 16)),
                                           np.broadcast_to(sk[1::2][None], (64, 16))], 0).reshape(128, 16))
    key = (_NT, _DO_SAMPLE)
    if key not in _CACHE:
        _CACHE[key] = build(_NT, _DO_SAMPLE)
    nc = _CACHE[key]
    in_maps = []
    for b in range(_cores):
        cc = np.stack([f(c_prompt)[b], f(c_sample)[b]], axis=-1)
        cT = np.ascontiguousarray(cc.reshape(16, 128, 2).transpose(1, 0, 2).reshape(128, 32))
        in_maps.append({
            "xp": x_prompt[b], "xs": x_sample[b], "cT": cT,
            "cak": f(cache_a_k)[0, b].reshape(512, 2048), "cav": f(cache_a_v)[0, b].reshape(512, 2048),
            "cbk": f(cache_b_k)[0, b].reshape(128, 512), "cbv": f(cache_b_v)[0, b].reshape(128, 512),
            "wall": wall, "lnT": lnT, "badaT": badaT, "G": G, "tabc": tabc, "sinkT": sinkT,
            "ident": ident, "permT": permT, "ropeC": ropeC, "ropeS": ropeS,
        })
    res = run_bass_kernel_spmd(nc, in_maps, core_ids=list(range(_cores)))
    R = res.results
    st = lambda n: np.stack([np.asarray(r[n], dtype=np.float32) for r in R], 0)
    y_p = st("y_p")
    y_s = st("y_s")
    ak_p = st("ak_p").reshape(_cores, 512, 16, 128)[None]
    av_p = st("av_p").reshape(_cores, 512, 16, 128)[None]
    bk_p = st("bk_p").reshape(_cores, 128, 8, 64)[None]
    bv_p = st("bv_p").reshape(_cores, 128, 8, 64)[None]
    ak_s = st("ak_s").reshape(_cores, 16, 16, 128)[None]
    av_s = st("av_s").reshape(_cores, 16, 16, 128)[None]
    bk_s = st("bk_s").reshape(_cores, 16, 8, 64)[None]
    bv_s = st("bv_s").reshape(_cores, 16, 8, 64)[None]
    return (y_p, y_s, ak_p, av_p, bk_p, bv_p, ak_s, av_s, bk_s, bv_s)
```
